# Optimizing a Trainium2 kernel written in Bass

```python
import math
import jax, jax.numpy as jnp
from jax import lax
import numpy as np

D_MODEL = 1024
BATCH = 16
SEQ = 2048
DEPTH = 4

GRID_W = 64
CTX_LEN = 256
HEAD_DIM = 64
N_FREQ = HEAD_DIM // 4
ROPE_BASE = 10000.0
Q_BLOCK = 128
BRANCH_WIDTH = D_MODEL // 2
N_BRANCHES = 3
A_Q_HEADS = BRANCH_WIDTH // HEAD_DIM
A_KV_HEADS = 2
A_GROUP = A_Q_HEADS // A_KV_HEADS
B_DV = 128
B_HEADS = BRANCH_WIDTH // B_DV
B_DK = 64
GLA_RANK = 16
GLA_TAU = 16.0
GLA_CHUNK = 64
C_HEADS = BRANCH_WIDTH // (2 * HEAD_DIM)
FFN_DIM = 2816
CONV_WIDTH = 3
ADA_CHUNKS = 6
EPS = 1e-6

PROJ_SIZES = (
    A_Q_HEADS * HEAD_DIM, A_KV_HEADS * HEAD_DIM, A_KV_HEADS * HEAD_DIM,
    B_HEADS * B_DK, B_HEADS * B_DK, B_HEADS * B_DV, B_HEADS * B_DV,
    2 * GLA_RANK,
    C_HEADS * 2 * HEAD_DIM, C_HEADS * 2 * HEAD_DIM, C_HEADS * 2 * HEAD_DIM,
    N_BRANCHES * D_MODEL,
)
PROJ_TOTAL = sum(PROJ_SIZES)
SPLIT_POINTS = tuple(int(v) for v in np.cumsum(PROJ_SIZES)[:-1])

kernel_name = 'hybrid_gated_parallel_mixer_dit_block'


def rms_norm(x, g):
    xf = x.astype(jnp.float32)
    y = xf * lax.rsqrt(jnp.mean(xf * xf, axis=-1, keepdims=True) + EPS)
    return (y * g.astype(jnp.float32)).astype(x.dtype)


def axial_rope_tables(rows):
    row = jnp.repeat(jnp.arange(rows, dtype=jnp.float32), GRID_W)
    col = jnp.tile(jnp.arange(GRID_W, dtype=jnp.float32), rows)
    inv_freq = ROPE_BASE ** (-jnp.arange(N_FREQ, dtype=jnp.float32) / N_FREQ)
    ang = jnp.concatenate([row[:, None] * inv_freq, col[:, None] * inv_freq], axis=-1)
    return jnp.cos(ang), jnp.sin(ang)


def apply_axial_rope(x, cos, sin):
    xa = x.reshape(x.shape[:-1] + (2, 2, N_FREQ))
    bshape = (x.shape[1],) + (1,) * (x.ndim - 3) + (2, 1, N_FREQ)
    c = cos.reshape(bshape).astype(x.dtype)
    s = sin.reshape(bshape).astype(x.dtype)
    x1 = xa[..., 0:1, :]
    x2 = xa[..., 1:2, :]
    rot = jnp.concatenate([-x2, x1], axis=-2)
    return (xa * c + rot * s).reshape(x.shape)


def split_projection(z):
    b, s, _ = z.shape
    qa, ka, va, qb, kb, vb, gb, rb, qc, kc, vc, gt = jnp.split(z, SPLIT_POINTS, axis=-1)
    qa = qa.reshape(b, s, A_KV_HEADS, A_GROUP, HEAD_DIM)
    ka = ka.reshape(b, s, A_KV_HEADS, HEAD_DIM)
    va = va.reshape(b, s, A_KV_HEADS, HEAD_DIM)
    qb = qb.reshape(b, s, B_HEADS, B_DK)
    kb = kb.reshape(b, s, B_HEADS, B_DK)
    vb = vb.reshape(b, s, B_HEADS, B_DV)
    gb = gb.reshape(b, s, B_HEADS, B_DV)
    qc = qc.reshape(b, s, C_HEADS, 2, HEAD_DIM)
    kc = kc.reshape(b, s, C_HEADS, 2, HEAD_DIM)
    vc = vc.reshape(b, s, C_HEADS, 2 * HEAD_DIM)
    return qa, ka, va, qb, kb, vb, gb, rb, qc, kc, vc, gt


def dual_attention(qa, ka, va, qc, kc, vc, lam):
    sa = jnp.einsum('bqhgd,bkhd->bhgqk', qa, ka).astype(jnp.float32)
    pa = jax.nn.softmax(sa, axis=-1).astype(va.dtype)
    oa = jnp.einsum('bhgqk,bkhd->bqhgd', pa, va)
    sc = jnp.einsum('bqhmd,bkhmd->bhmqk', qc, kc).astype(jnp.float32)
    pc = jax.nn.softmax(sc, axis=-1)
    pd = (pc[:, :, 0] - lam * pc[:, :, 1]).astype(vc.dtype)
    oc = jnp.einsum('bhqk,bkhe->bqhe', pd, vc)
    return oa, oc


def to_blocks(t):
    b, s = t.shape[:2]
    return jnp.moveaxis(t.reshape((b, s // Q_BLOCK, Q_BLOCK) + t.shape[2:]), 1, 0)


def from_blocks(t):
    t = jnp.moveaxis(t, 0, 1)
    return t.reshape((t.shape[0], -1) + t.shape[3:])


def gla_log_decay(r, w_dec, b_dec):
    b, s, _ = r.shape
    r_f, r_b = jnp.split(r, 2, axis=-1)
    def one(rr, w, bias):
        z = (rr @ w + bias).astype(jnp.float32)
        return (jax.nn.log_sigmoid(z) / GLA_TAU).reshape(b, s, B_HEADS, B_DK)
    return one(r_f, w_dec[0], b_dec[0]), one(r_b, w_dec[1], b_dec[1])


def gla_chunked(q, k, v, g, s0):
    b, h, s, dk = q.shape
    dv = v.shape[-1]
    n = s // GLA_CHUNK
    q = q.reshape(b, h, n, GLA_CHUNK, dk)
    k = k.reshape(b, h, n, GLA_CHUNK, dk)
    v = v.reshape(b, h, n, GLA_CHUNK, dv)
    G = jnp.cumsum(g.reshape(b, h, n, GLA_CHUNK, dk), axis=3)
    G_last = G[..., -1:, :]
    q_d = q * jnp.exp(G)
    k_d = k * jnp.exp(-G)
    k_end = k * jnp.exp(G_last - G)
    causal = jnp.tril(jnp.ones((GLA_CHUNK, GLA_CHUNK), dtype=bool))
    att = jnp.where(causal, jnp.einsum('bhnid,bhnjd->bhnij', q_d, k_d), 0.0)
    o_intra = jnp.einsum('bhnij,bhnje->bhnie', att, v)
    u = jnp.einsum('bhncd,bhnce->bhnde', k_end, v)
    decay = jnp.exp(G_last[..., 0, :])
    def step(state, inp):
        dec, inc = inp
        return dec[..., None] * state + inc, state
    s_fin, s_start = lax.scan(step, s0, (jnp.moveaxis(decay, 2, 0), jnp.moveaxis(u, 2, 0)))
    s_start = jnp.moveaxis(s_start, 0, 2)
    o_inter = jnp.einsum('bhncd,bhnde->bhnce', q_d, s_start)
    return (o_intra + o_inter).reshape(b, h, s, dv), s_fin


def gla_bidirectional(q, k, v, la_f, la_b, s0_f, s0_b):
    heads_first = lambda a: jnp.swapaxes(a, 1, 2).astype(jnp.float32)
    rev = lambda a: jnp.flip(a, axis=2)
    qh, kh, vh, gf, gb = (heads_first(a) for a in (q, k, v, la_f, la_b))
    o_f, s_f = gla_chunked(qh, kh, vh, gf, s0_f)
    o_b, s_b = gla_chunked(rev(qh), rev(kh), rev(vh), rev(gb), s0_b)
    o = jnp.swapaxes(o_f + rev(o_b), 1, 2).astype(v.dtype)
    return o, s_f, s_b


def merge_branches(ya, yb, yc, gates, w_br, w_o):
    b, s, _ = gates.shape
    g = jax.nn.sigmoid(gates.reshape(b, s, N_BRANCHES, D_MODEL))
    y = jnp.stack([ya, yb, yc], axis=2)
    br = jnp.einsum('bsnc,ncd->bsnd', y, w_br)
    return jnp.einsum('bsd,de->bse', jnp.sum(g * br, axis=2), w_o)


def branch_outputs(oa, ob, gb, oc, gla_g, diff_g, lambda_init):
    b, s = oa.shape[:2]
    ya = oa.reshape(b, s, BRANCH_WIDTH)
    yb = (rms_norm(ob, gla_g) * jax.nn.silu(gb)).reshape(b, s, BRANCH_WIDTH)
    yc = (rms_norm(oc, diff_g) * (1.0 - lambda_init)).reshape(b, s, BRANCH_WIDTH)
    return ya, yb, yc


def token_mixer(u_ctx, u_lat, w_in, qk_g, w_dec, b_dec, gla_g, lam_p, diff_g, w_br, w_o,
                cos, sin, lambda_init, need_ctx):
    qa_c, ka_c, va_c, qb_c, kb_c, vb_c, gb_c, rb_c, qc_c, kc_c, vc_c, gt_c = split_projection(u_ctx @ w_in)
    qa_l, ka_l, va_l, qb_l, kb_l, vb_l, gb_l, rb_l, qc_l, kc_l, vc_l, gt_l = split_projection(u_lat @ w_in)
    scale = HEAD_DIM ** -0.5
    qa_c = rms_norm(qa_c, qk_g[0]) * scale
    ka_c = rms_norm(ka_c, qk_g[1])
    qa_l = apply_axial_rope(rms_norm(qa_l, qk_g[0]), cos, sin) * scale
    ka_l = apply_axial_rope(rms_norm(ka_l, qk_g[1]), cos, sin)
    qc_c = qc_c * scale
    qc_l = apply_axial_rope(qc_l, cos, sin) * scale
    kc_l = apply_axial_rope(kc_l, cos, sin)
    lp = lam_p.astype(jnp.float32)
    lam = jnp.exp(jnp.sum(lp[0] * lp[1])) - jnp.exp(jnp.sum(lp[2] * lp[3])) + lambda_init
    ka_all = jnp.concatenate([ka_c, ka_l], axis=1)
    va_all = jnp.concatenate([va_c, va_l], axis=1)
    kc_all = jnp.concatenate([kc_c, kc_l], axis=1)
    vc_all = jnp.concatenate([vc_c, vc_l], axis=1)
    oa_l, oc_l = lax.map(
        lambda qs: dual_attention(qs[0], ka_all, va_all, qs[1], kc_all, vc_all, lam),
        (to_blocks(qa_l), to_blocks(qc_l)))
    oa_l, oc_l = from_blocks(oa_l), from_blocks(oc_l)
    sb = B_DK ** -0.5
    laf_c, lab_c = gla_log_decay(rb_c, w_dec, b_dec)
    laf_l, lab_l = gla_log_decay(rb_l, w_dec, b_dec)
    zero_state = jnp.zeros((u_ctx.shape[0], B_HEADS, B_DK, B_DV), jnp.float32)
    ob_c, s_f, s_b = gla_bidirectional(qb_c * sb, kb_c, vb_c, laf_c, lab_c, zero_state, zero_state)
    ob_l, _, _ = gla_bidirectional(qb_l * sb, kb_l, vb_l, laf_l, lab_l, s_f, s_b)
    ya, yb, yc = branch_outputs(oa_l, ob_l, gb_l, oc_l, gla_g, diff_g, lambda_init)
    y_lat = merge_branches(ya, yb, yc, gt_l, w_br, w_o)
    if not need_ctx:
        return None, y_lat
    oa_c, oc_c = dual_attention(qa_c, ka_c, va_c, qc_c, kc_c, vc_c, lam)
    ya, yb, yc = branch_outputs(oa_c, ob_c, gb_c, oc_c, gla_g, diff_g, lambda_init)
    y_ctx = merge_branches(ya, yb, yc, gt_c, w_br, w_o)
    return y_ctx, y_lat


def depthwise_conv(a, w, b):
    out = lax.conv_general_dilated(
        a, w[:, None, :], window_strides=(1,),
        padding=((CONV_WIDTH // 2, CONV_WIDTH // 2),),
        dimension_numbers=('NWC', 'WIO', 'NWC'),
        feature_group_count=a.shape[-1])
    return out + b


def conv_ffn(u, w_up, conv_w, conv_b, w_down):
    gate, val = jnp.split(u @ w_up, 2, axis=-1)
    gate = depthwise_conv(gate, conv_w, conv_b)
    return (jax.nn.silu(gate) * val) @ w_down


def setup_inputs(seed: int = 0) -> dict:
    key = jax.random.key(seed)
    ks = jax.random.split(key, 20)
    nrm = jax.random.normal
    f = jnp.float32
    return {
        'x': nrm(ks[0], (BATCH, SEQ, D_MODEL), f),
        'c': nrm(ks[1], (BATCH, D_MODEL), f),
        'ctx': nrm(ks[2], (BATCH, CTX_LEN, D_MODEL), f),
        'c_ctx': nrm(ks[3], (D_MODEL,), f),
        'ada_w': nrm(ks[4], (DEPTH, D_MODEL, ADA_CHUNKS * D_MODEL), f) * (0.5 * D_MODEL ** -0.5),
        'ada_b': 0.02 * nrm(ks[5], (DEPTH, ADA_CHUNKS * D_MODEL), f),
        'norm_g': 1.0 + 0.05 * nrm(ks[6], (DEPTH, 4, D_MODEL), f),
        'w_in': nrm(ks[7], (DEPTH, D_MODEL, PROJ_TOTAL), f) * D_MODEL ** -0.5,
        'qk_norm_a': 1.0 + 0.05 * nrm(ks[8], (DEPTH, 2, HEAD_DIM), f),
        'gla_w_decay': nrm(ks[9], (DEPTH, 2, GLA_RANK, B_HEADS * B_DK), f) * GLA_RANK ** -0.5,
        'gla_b_decay': 0.1 * nrm(ks[10], (DEPTH, 2, B_HEADS * B_DK), f),
        'gla_norm': 1.0 + 0.05 * nrm(ks[11], (DEPTH, B_DV), f),
        'diff_lambda': 0.1 * nrm(ks[12], (DEPTH, 4, HEAD_DIM), f),
        'diff_norm': 1.0 + 0.05 * nrm(ks[13], (DEPTH, 2 * HEAD_DIM), f),
        'w_branch': nrm(ks[14], (DEPTH, N_BRANCHES, BRANCH_WIDTH, D_MODEL), f) * BRANCH_WIDTH ** -0.5,
        'w_out': nrm(ks[15], (DEPTH, D_MODEL, D_MODEL), f) * D_MODEL ** -0.5,
        'w_ffn_in': nrm(ks[16], (DEPTH, D_MODEL, 2 * FFN_DIM), f) * D_MODEL ** -0.5,
        'ffn_conv_w': nrm(ks[17], (DEPTH, CONV_WIDTH, FFN_DIM), f) * CONV_WIDTH ** -0.5,
        'ffn_conv_b': 0.02 * nrm(ks[18], (DEPTH, FFN_DIM), f),
        'w_ffn_out': nrm(ks[19], (DEPTH, FFN_DIM, D_MODEL), f) * FFN_DIM ** -0.5,
    }


def reference(x, c, ctx, c_ctx, ada_w, ada_b, norm_g, w_in, qk_norm_a, gla_w_decay, gla_b_decay,
              gla_norm, diff_lambda, diff_norm, w_branch, w_out, w_ffn_in, ffn_conv_w, ffn_conv_b,
              w_ffn_out):
    rows = x.shape[1] // GRID_W
    cos, sin = axial_rope_tables(rows)
    silu_c = jax.nn.silu(c)
    silu_cc = jax.nn.silu(c_ctx)
    h_lat, h_ctx = x, ctx
    for layer in range(DEPTH):
        need_ctx = layer < DEPTH - 1
        lambda_init = 0.8 - 0.6 * math.exp(-0.3 * layer)
        sh1, sc1, g1, sh2, sc2, g2 = jnp.split((silu_c @ ada_w[layer] + ada_b[layer])[:, None, :], ADA_CHUNKS, axis=-1)
        csh1, csc1, cg1, csh2, csc2, cg2 = jnp.split(silu_cc @ ada_w[layer] + ada_b[layer], ADA_CHUNKS, axis=-1)
        g_pre1, g_post1, g_pre2, g_post2 = norm_g[layer]
        u_lat = rms_norm(h_lat, g_pre1) * (1.0 + sc1) + sh1
        u_ctx = rms_norm(h_ctx, g_pre1) * (1.0 + csc1) + csh1
        y_ctx, y_lat = token_mixer(
            u_ctx, u_lat, w_in[layer], qk_norm_a[layer], gla_w_decay[layer], gla_b_decay[layer],
            gla_norm[layer], diff_lambda[layer], diff_norm[layer], w_branch[layer], w_out[layer],
            cos, sin, lambda_init, need_ctx)
        h_lat = h_lat + g1 * rms_norm(y_lat, g_post1)
        v_lat = rms_norm(h_lat, g_pre2) * (1.0 + sc2) + sh2
        h_lat = h_lat + g2 * rms_norm(conv_ffn(v_lat, w_ffn_in[layer], ffn_conv_w[layer], ffn_conv_b[layer], w_ffn_out[layer]), g_post2)
        if need_ctx:
            h_ctx = h_ctx + cg1 * rms_norm(y_ctx, g_post1)
            v_ctx = rms_norm(h_ctx, g_pre2) * (1.0 + csc2) + csh2
            h_ctx = h_ctx + cg2 * rms_norm(conv_ffn(v_ctx, w_ffn_in[layer], ffn_conv_w[layer], ffn_conv_b[layer], w_ffn_out[layer]), g_post2)
    return h_lat
```

```python
import math
import os
STOP = float(os.environ.get("KSTOP", "9"))
SKIP = os.environ.get("KSKIP", "").split(",")
from contextlib import ExitStack
import numpy as np
import concourse.bass as bass
import concourse.mybir as mybir
from concourse.bass_utils import run_bass_kernel_spmd

F32 = mybir.dt.float32
BF16 = mybir.dt.bfloat16
U8 = mybir.dt.uint8
AF = mybir.ActivationFunctionType
ALU = mybir.AluOpType

D = 1024
KC = 8
NB = 256
NBLK = 9
T = 2304
DEPTH = 4
EPS = 1e-6
FFN = 2816
FC = 22
LI = [0.8 - 0.6 * math.exp(-0.3 * l) for l in range(DEPTH)]


class V:
    def __init__(self, ap, keys):
        self.ap = ap
        self.keys = tuple(keys)

    def __getitem__(self, idx):
        return V(self.ap[idx], self.keys)

    def k(self, *keys):
        return V(self.ap, keys)


class Op:
    __slots__ = ("eng", "fn", "reads", "writes", "dma", "semkey", "barrier", "deps", "signal", "val", "waits", "full")


class Prog:
    ENGS = ("pe", "act", "dve", "pool", "sp")

    def __init__(self):
        self.ops = []
        self.dry = False

    def add(self, eng, fn, ins=(), outs=(), dma=False, semkey=None, barrier=False, full=False):
        if self.dry:
            return
        op = Op()
        op.full = full
        op.eng = eng
        op.fn = fn
        r = []
        for v in ins:
            if isinstance(v, V):
                r.extend(v.keys)
        w = []
        for v in outs:
            w.extend(v.keys)
        op.reads = r
        op.writes = w
        op.dma = dma
        op.semkey = semkey
        op.barrier = barrier
        self.ops.append(op)

    def mm(self, out, lhsT, rhs, start=True, stop=True):
        self.add("pe", lambda e: e.matmul(out.ap, lhsT.ap, rhs.ap, start=start, stop=stop), [lhsT, rhs], [out])

    def act(self, out, in_, func, scale=1.0, bias=None):
        b = bias.ap if isinstance(bias, V) else bias
        if b is None:
            self.add("act", lambda e: e.activation(out=out.ap, in_=in_.ap, func=func, scale=scale), [in_], [out])
        else:
            self.add("act", lambda e: e.activation(out=out.ap, in_=in_.ap, func=func, scale=scale, bias=b),
                     [in_, bias], [out])

    def ts(self, eng, out, in0, s1, s2, op0, op1=None):
        a1 = s1.ap if isinstance(s1, V) else s1
        a2 = s2.ap if isinstance(s2, V) else s2
        if op1 is None:
            f = lambda e: e.tensor_scalar(out=out.ap, in0=in0.ap, scalar1=a1, scalar2=None, op0=op0)
        else:
            f = lambda e: e.tensor_scalar(out=out.ap, in0=in0.ap, scalar1=a1, scalar2=a2, op0=op0, op1=op1)
        self.add(eng, f, [in0, s1, s2], [out])

    def stt(self, eng, out, in0, scalar, in1, op0, op1):
        a = scalar.ap if isinstance(scalar, V) else scalar
        self.add(eng, lambda e: e.scalar_tensor_tensor(out=out.ap, in0=in0.ap, scalar=a, in1=in1.ap, op0=op0, op1=op1),
                 [in0, scalar, in1], [out])

    def tt(self, eng, out, in0, in1, op):
        self.add(eng, lambda e: e.tensor_tensor(out=out.ap, in0=in0.ap, in1=in1.ap, op=op), [in0, in1], [out])

    def copy(self, eng, out, in_):
        if eng == "act":
            self.add("act", lambda e: e.copy(out=out.ap, in_=in_.ap), [in_], [out])
        else:
            self.add(eng, lambda e: e.tensor_copy(out=out.ap, in_=in_.ap), [in_], [out])

    def recip(self, out, in_):
        self.add("dve", lambda e: e.reciprocal(out=out.ap, in_=in_.ap), [in_], [out])

    def memset(self, eng, out, val):
        self.add(eng, lambda e: e.memset(out.ap, val), [], [out])

    def dma(self, eng, out, in_, semkey):
        self.add(eng, lambda e: e.dma_start(out=out.ap, in_=in_.ap), [in_], [out], dma=True, semkey=semkey)

    def barrier(self, scratch, full=False):
        self.add("pool", lambda e: e.memset(scratch.ap, 0.0), [], [scratch], barrier=True, full=full)

    def analyze(self):
        ops = self.ops
        lastw = {}
        readers = {}
        last_on = {}
        dma_cnt = {}
        sig_cnt = {e: 0 for e in self.ENGS}
        bar_idx = None
        synced = set()
        for i, op in enumerate(ops):
            deps = {}
            if op.barrier:
                for e, j in last_on.items():
                    deps[j] = "raw"
                op.deps = (deps, {k_: v_ for k_, v_ in dma_cnt.items() if op.full or not (isinstance(k_, tuple) and k_[0] == "wbf")})
                bar_idx = i
                synced = {op.eng}
            else:
                for r in op.reads:
                    j = lastw.get(r)
                    if j is not None:
                        deps[j] = "raw"
                for w in op.writes:
                    j = lastw.get(w)
                    if j is not None and j not in deps:
                        deps[j] = "waw"
                    for j2 in readers.get(w, ()):
                        if j2 not in deps:
                            deps[j2] = "war"
                if bar_idx is not None and op.eng not in synced:
                    deps[bar_idx] = "raw"
                    synced.add(op.eng)
                op.deps = (deps, None)
            for r in op.reads:
                readers.setdefault(r, []).append(i)
            for w in op.writes:
                lastw[w] = i
                readers[w] = []
            last_on[op.eng] = i
            op.signal = op.dma
            if op.dma:
                dma_cnt[op.semkey] = dma_cnt.get(op.semkey, 0) + 1
            op.waits = None
        for i, op in enumerate(ops):
            deps, _ = op.deps
            keep = []
            for j, kind in deps.items():
                pj = ops[j]
                if j == i:
                    continue
                if pj.eng == op.eng and not pj.dma:
                    if op.eng == "pe":
                        continue
                keep.append(j)
                if not pj.dma:
                    pj.signal = True
            op.deps = (keep, op.deps[1])
        cnt = {e: 0 for e in self.ENGS}
        dcnt = {}
        snap = []
        for i, op in enumerate(ops):
            snap.append(None)
            if op.dma:
                dcnt[op.semkey] = dcnt.get(op.semkey, 0) + 1
                op.val = None
            elif op.signal:
                cnt[op.eng] += 1
                op.val = cnt[op.eng]
            else:
                op.val = None
        dcnt = {}
        for i, op in enumerate(ops):
            keep, bar = op.deps
            waits = {}
            for j in keep:
                pj = ops[j]
                if pj.dma:
                    key = ("d", pj.semkey)
                    val = 16 * dcnt[pj.semkey]
                else:
                    key = ("e", pj.eng)
                    val = pj.val
                if waits.get(key, 0) < val:
                    waits[key] = val
            if bar is not None:
                for sk, c in bar.items():
                    key = ("d", sk)
                    if waits.get(key, 0) < 16 * c:
                        waits[key] = 16 * c
            op.waits = waits
            if op.dma:
                dcnt[op.semkey] = dcnt.get(op.semkey, 0) + 1
        self.semkeys = list(dcnt.keys())

    def emit(self, nc):
        self.analyze()
        ops = self.ops
        with ExitStack() as es:
            sems = {}
            for e in self.ENGS:
                sems[("e", e)] = es.enter_context(nc.semaphore("se_" + e))
            for n, sk in enumerate(self.semkeys):
                sems[("d", sk)] = es.enter_context(nc.semaphore("sd%d" % n))
            block = es.enter_context(nc.Block())
            by = {e: [op for op in ops if op.eng == e] for e in self.ENGS}

            def run(engname, e):
                waited = {}
                for op in by[engname]:
                    for key, val in op.waits.items():
                        if waited.get(key, 0) >= val:
                            continue
                        waited[key] = val
                        e.wait_ge(sems[key], val)
                    ins = op.fn(e)
                    if op.dma:
                        ins.then_inc(sems[("d", op.semkey)], 16)
                    elif op.signal:
                        ins.then_inc(sems[("e", engname)], 1)

            @block.tensor
            def _(e):
                run("pe", e)

            @block.scalar
            def _(e):
                run("act", e)

            @block.vector
            def _(e):
                run("dve", e)

            @block.gpsimd
            def _(e):
                run("pool", e)

            @block.sync
            def _(e):
                run("sp", e)


def _kcl(W):
    K, C = W.shape
    return np.ascontiguousarray(W.reshape(K // 128, 128, C).transpose(1, 0, 2)).reshape(128, -1)


def weight_groups():
    g = []
    for i in range(12):
        g.append(("ada%d" % i, 8 * 512))
    for hp in range(2):
        g.append(("g1_%d" % hp, 8 * 288))
        g.append(("g2_%d" % hp, 8 * 512))
    g += [("kvc1", 8 * 512), ("kvc2", 8 * 512), ("qc", 8 * 512), ("kva", 8 * 256), ("qa", 8 * 512)]
    for m in range(8):
        g.append(("mg%d" % m, 1536 + 3072))
    g += [("wo0", 8 * 512), ("wo1", 8 * 512)]
    for i in range(11):
        g.append(("up%d" % i, 8 * 512))
    for m in range(8):
        g.append(("dn%d" % m, 22 * 128))
    return g


WG = weight_groups()
WOFF = {}
_o = 0
for _n, _e in WG:
    WOFF[_n] = (_o, _e)
    _o += _e
WE = _o
SLOT = 4608


def prep_layer(inp, l):
    w_in = inp["w_in"][l]
    pieces = []
    ada = inp["ada_w"][l]
    for i in range(12):
        pieces.append(_kcl(ada[:, i * 512:(i + 1) * 512]))
    for hp in range(2):
        cols = np.concatenate([np.arange(768 + hp * 128, 768 + hp * 128 + 128),
                               np.arange(1024 + hp * 128, 1024 + hp * 128 + 128),
                               np.arange(2304, 2336)])
        pieces.append(_kcl(w_in[:, cols]))
        cols = np.concatenate([np.arange(1280 + hp * 256, 1280 + hp * 256 + 256),
                               np.arange(1792 + hp * 256, 1792 + hp * 256 + 256)])
        pieces.append(_kcl(w_in[:, cols]))
    pieces.append(_kcl(w_in[:, 2848:3360]))
    pieces.append(_kcl(w_in[:, 3360:3872]))
    pieces.append(_kcl(w_in[:, 2336:2848]))
    pieces.append(_kcl(w_in[:, 512:768]))
    cols = []
    for c in range(4):
        cols += list(range(c * 64, c * 64 + 64)) + list(range((4 + c) * 64, (4 + c) * 64 + 64))
    pieces.append(_kcl(w_in[:, np.array(cols)]))
    wbr = inp["w_branch"][l]
    for m in range(8):
        a = wbr[:, :, m * 128:(m + 1) * 128].reshape(3, 4, 128, 128).transpose(2, 0, 1, 3).reshape(128, -1)
        gcols = np.concatenate([np.arange(3872 + n * 1024 + m * 128, 3872 + n * 1024 + m * 128 + 128) for n in range(3)])
        b = w_in[:, gcols].reshape(8, 128, 3, 128).transpose(1, 2, 0, 3).reshape(128, -1)
        pieces.append(np.concatenate([a, b], axis=1))
    wo = inp["w_out"][l]
    pieces.append(_kcl(wo[:, 0:512]))
    pieces.append(_kcl(wo[:, 512:1024]))
    wu = inp["w_ffn_in"][l]
    for i in range(11):
        cols = np.concatenate([np.arange(j * 128, j * 128 + 128) if t == 0 else np.arange(FFN + j * 128, FFN + j * 128 + 128)
                               for j in (2 * i, 2 * i + 1) for t in (0, 1)])
        pieces.append(_kcl(wu[:, cols]))
    wd = inp["w_ffn_out"][l]
    for m in range(8):
        pieces.append(_kcl(wd[:, m * 128:(m + 1) * 128]))
    W = np.concatenate(pieces, axis=1)
    assert W.shape == (128, WE), W.shape
    return np.ascontiguousarray(W, dtype=np.float32)


SM = {}
_o = 0
for _n, _e in [("adab", 4 * 48), ("ng", 4 * 4 * 8), ("qkg", 4 * 2), ("glag", 4), ("dgn", 4), ("convw", 4 * 22 * 3),
               ("convb", 4 * 22), ("lp", 4 * 4 * 64), ("cT", 8 * 3)]:
    SM[_n] = (_o, _e)
    _o += _e
NSM = _o
NMAT = 14


def prep_consts(inp):
    sm = np.zeros((128, NSM), np.float32)

    def put(name, arr):
        o, e = SM[name]
        sm[:, o:o + e] = arr.reshape(128, e)

    put("adab", inp["ada_b"].reshape(4, 48, 128).transpose(2, 0, 1))
    put("ng", inp["norm_g"].reshape(4, 4, 8, 128).transpose(3, 0, 1, 2))
    put("qkg", np.tile(inp["qk_norm_a"].transpose(2, 0, 1), (2, 1, 1)))
    put("glag", inp["gla_norm"].T)
    put("dgn", inp["diff_norm"].T)
    put("convw", inp["ffn_conv_w"].reshape(4, 3, 22, 128).transpose(3, 0, 2, 1))
    put("convb", inp["ffn_conv_b"].reshape(4, 22, 128).transpose(2, 0, 1))
    put("lp", np.broadcast_to(inp["diff_lambda"].reshape(1, -1), (128, 1024)))
    cm = np.zeros((128, NMAT, 128), np.float32)
    cm[:, 0] = 1.0 / 1024
    cm[:, 1] = 0.25 / 1024
    cm[:, 2] = 1.0 / 128
    idx = np.arange(128)
    same64 = (idx[:, None] // 64) == (idx[None, :] // 64)
    cm[:, 3] = same64 / 64.0
    cm[:, 4] = 1.0
    P = np.zeros((128, 128), np.float32)
    for p in range(128):
        half = (p % 32) // 16
        if half == 0:
            P[p + 16, p] = -1.0
        else:
            P[p - 16, p] = 1.0
    cm[:, 5] = P
    l_ = idx[:, None]
    t_ = idx[None, :]
    cm[:, 6] = same64 & (l_ <= t_)
    cm[:, 7] = same64 & (l_ >= t_)
    cm[:, 8] = same64 & (l_ > t_)
    cm[:, 9] = same64 & (l_ < t_)
    cm[:, 10] = cm[:, 6]
    cm[:, 11] = cm[:, 7]
    cm[:, 12] = cm[:, 6]
    cm[:, 13] = cm[:, 7]
    d = idx % 64
    axis = d // 32
    f = d % 16
    inv = (10000.0 ** (-np.arange(16, dtype=np.float32) / 16)).astype(np.float32)
    t = np.arange(2048)
    row = (t // 64).astype(np.float32)
    col = (t % 64).astype(np.float32)
    pos = np.where(axis[:, None] == 0, row[None, :], col[None, :]).astype(np.float32)
    ang = (pos * inv[f][:, None]).astype(np.float32)
    rope = np.concatenate([np.cos(ang), np.sin(ang)], axis=1).astype(np.float32)
    w2 = np.zeros((64, 4, 2, 2, 2, 64), np.float32)
    wd = inp["gla_w_decay"].reshape(4, 2, 16, 4, 64)
    bd = inp["gla_b_decay"].reshape(4, 2, 4, 64)
    for dr in range(2):
        for hp in range(2):
            for hh in range(2):
                w2[dr * 16:(dr + 1) * 16, :, hp, dr, hh, :] = wd[:, dr, :, 2 * hp + hh, :].transpose(1, 0, 2)
                w2[32, :, hp, dr, hh, :] = bd[:, dr, 2 * hp + hh, :]
    w2 = w2.reshape(64, 2048)
    return sm, cm.reshape(128, NMAT * 128), rope, w2


class Mem:
    def __init__(self, big, cap):
        self.big = big
        self.cap = cap
        self.off = 0

    def alloc(self, name, shape, dtype, nparts=128):
        esz = 4 if dtype == F32 else 2
        n = 1
        for s in shape:
            n *= s
        nb = (n * esz + 31) // 32 * 32
        assert self.off + nb <= self.cap, ("SBUF overflow", name, self.off, nb, self.cap)
        ap = self.big[0:nparts, self.off:self.off + n * esz].bitcast(dtype)
        if len(shape) == 2:
            ap = ap.rearrange("p (a b) -> p a b", a=shape[0])
        elif len(shape) == 3:
            ap = ap.rearrange("p (a b c) -> p a b c", a=shape[0], b=shape[1])
        elif len(shape) == 4:
            ap = ap.rearrange("p (a b c d) -> p a b c d", a=shape[0], b=shape[1], c=shape[2])
        self.off += nb
        return V(ap, [name])


def build(NL=DEPTH, NSEQ=2):
    nc = bass.Bass("TRN2", target_bir_lowering=False)
    xT = nc.dram_tensor("xT", [NSEQ, 128, KC, 2048], F32, kind="ExternalInput").ap()
    cxT = nc.dram_tensor("cxT", [NSEQ, 128, KC, 256], F32, kind="ExternalInput").ap()
    smd = nc.dram_tensor("smallc", [128, NSM], F32, kind="ExternalInput").ap()
    cmd = nc.dram_tensor("cmat", [128, NMAT * 128], F32, kind="ExternalInput").ap()
    roped = nc.dram_tensor("rope", [128, 4096], F32, kind="ExternalInput").ap()
    w2d = nc.dram_tensor("w2aug", [64, 2048], F32, kind="ExternalInput").ap()
    Wd = nc.dram_tensor("W", [NL, 128, WE], F32, kind="ExternalInput").ap()
    outd = nc.dram_tensor("out", [NSEQ, 128, KC, 2048], F32, kind="ExternalOutput").ap()
    wbf = nc.dram_tensor("wbf", [NL, 128, WE], BF16, kind="Internal").ap()
    ud = nc.dram_tensor("ud", [128, NBLK, KC * NB], BF16, kind="Internal").ap()
    yd = nc.dram_tensor("yd", [128, 3, NBLK, 4 * NB], BF16, kind="Internal").ap()

    cap = (nc.sbuf_bytes_remaining - 1024) // 64 * 64
    LIM = cap - int(os.environ.get("KTOP", "0"))
    big = nc.alloc_sbuf_tensor("big", [128, cap], U8)
    M = Mem(big, LIM)
    PS = [V(nc.alloc_psum_tensor("ps%d" % i, [128, 512], F32)[:, :], [("ps", i)]) for i in range(8)]
    P = Prog()

    sm = M.alloc("sm", [NSM], F32)
    cm = M.alloc("cm", [NMAT, 128], BF16)
    w2 = M.alloc("w2", [2048], BF16)
    onesf = M.alloc("onesf", [128], F32)

    def smv(name, *shape):
        o, e = SM[name]
        ap = sm.ap[:, o:o + e]
        if len(shape) == 2:
            ap = ap.rearrange("p (a b) -> p a b", a=shape[0])
        elif len(shape) == 3:
            ap = ap.rearrange("p (a b c) -> p a b c", a=shape[0], b=shape[1])
        return V(ap, ["sm"])

    adab = smv("adab", 4, 48)
    ng = smv("ng", 4, 4, 8)
    qkg = smv("qkg", 4, 2)
    glag = smv("glag")
    dgn = smv("dgn")
    convw = smv("convw", 4, 22, 3)
    convb = smv("convb", 4, 22)
    lp = smv("lp", 4, 4, 64)
    cTv = smv("cT", 8, 3)
    scT = M.alloc("scT", [8, 3], BF16)
    modl = M.alloc("modl", [3, 48], F32)
    A1 = M.alloc("A1", [4, 3, 8], F32)
    B1 = M.alloc("B1", [4, 3, 8], F32)
    G1 = M.alloc("G1", [4, 3, 8], F32)
    A2 = M.alloc("A2", [4, 3, 8], F32)
    B2 = M.alloc("B2", [4, 3, 8], F32)
    G2 = M.alloc("G2", [4, 3, 8], F32)
    lamneg = M.alloc("lamneg", [4], F32)
    dgs = M.alloc("dgs", [4], F32)
    glagh = M.alloc("glagh", [4], F32)
    smt = M.alloc("smt", [4, 64], F32)
    barsc = M.alloc("barsc", [8], F32)
    h = M.alloc("h", [KC, T], F32)
    NSLOT = 2
    wsl = [M.alloc("wsl%d" % i, [SLOT], BF16) for i in range(NSLOT)]
    rstd = M.alloc("rstd", [512], F32)
    tmps = [M.alloc("tmp%d" % i, [512], F32) for i in range(4)]
    epsb = M.alloc("epsb", [1], F32)
    persist_end = M.off
    PH = {}

    def phase_begin(ntok, need_rope=False, need_sq=True):
        P.barrier(barsc)
        M.off = persist_end
        PH["uT"] = M.alloc("uT", [KC, ntok], BF16)
        PH.pop("uT2", None)
        if ntok == NB:
            PH["uT2"] = M.alloc("uT2", [KC, ntok], BF16)
        if need_sq:
            PH["sq"] = M.alloc("sq", [KC, ntok], BF16)
        if need_rope:
            PH["rope"] = M.alloc("rope", [4096], F32)
            P.dma("sp", PH["rope"], V(roped, ["roped"]), "c_rope")
    _tc = [0]

    def tmp():
        _tc[0] += 1
        return tmps[_tc[0] % 4]

    _pc = [0]

    def psr(lo=0, hi=8):
        _pc[0] += 1
        return PS[lo + _pc[0] % (hi - lo)]

    def hk(k, t0, n):
        keys = [("h", b) for b in range(t0 // NB, (t0 + n - 1) // NB + 1)]
        return V(h.ap[:, k, t0:t0 + n], keys)

    def mat(i):
        return V(cm.ap[:, i, :], ["cm"])

    class WS:
        def __init__(self):
            self.plan = []
            self.i = 0
            self.issued = 0

        def get(self, l, name):
            if P.dry:
                self.plan.append((l, name))
                return wsl[0]
            assert self.plan[self.i] == (l, name), (self.plan[self.i], l, name)
            while self.issued < min(len(self.plan), self.i + NSLOT):
                ll, nn = self.plan[self.issued]
                o, e = WOFF[nn]
                s = wsl[self.issued % NSLOT]
                pstep = (WE + 7) // 8
                pk = [("wbf", ll, i) for i in range(o // pstep, (o + e - 1) // pstep + 1)]
                P.dma("sp", V(s.ap[:, 0:e], s.keys), V(wbf[ll, :, o:o + e], pk), ("wsl", self.issued % NSLOT))
                self.issued += 1
            s = wsl[self.i % NSLOT]
            self.i += 1
            return s

    ws = WS()

    def wv(slot, kc, cols):
        return V(slot.ap[:, 0:kc * cols].rearrange("p (k c) -> p k c", k=kc), slot.keys)

    def rstd_from(ps_ms, n, out=None):
        o = out if out is not None else V(rstd.ap[:, 0:n], rstd.keys)
        t = tmp()
        tv = V(t.ap[:, 0:n], t.keys)
        P.act(tv, ps_ms, AF.Ln, bias=epsb)
        P.act(o, tv, AF.Exp, scale=-0.5)
        return o

    def norm_mod(t0, n, Am, Bm, l, col, dst):
        for k in range(KC):
            P.act(V(PH["sq"].ap[:, k, 0:n], [("sq", k)]), hk(k, t0, n), AF.Square)
        ps = psr(5, 8)
        for k in range(KC):
            P.mm(ps[:, 0:n], mat(0), V(PH["sq"].ap[:, k, 0:n], [("sq", k)]), start=(k == 0), stop=(k == KC - 1))
        r = rstd_from(ps[:, 0:n], n)
        for k in range(KC):
            t = tmp()
            tv = V(t.ap[:, 0:n], t.keys)
            P.stt("dve", tv, hk(k, t0, n), Am[:, l, col, k:k + 1], r, ALU.mult, ALU.mult)
            P.act(V(dst.ap[:, k, 0:n], [(dst.keys[0], k)]), tv, AF.Identity, bias=Bm[:, l, col, k:k + 1])

    def load_u(blk):
        PH["ui"] = PH.get("ui", 0) + 1
        uT = PH["uT"] if (PH["ui"] % 2 == 0 or "uT2" not in PH) else PH["uT2"]
        P.dma("sp", V(uT.ap[:, :, 0:NB], uT.keys), V(ud[:, blk, :].rearrange("p (k t) -> p k t", k=KC), ["ud"]),
              ("uT", uT.keys[0]))
        return uT

    def proj_fm(w, c0, nco, x, n, ps):
        for kc in range(KC):
            P.mm(ps[0:nco, 0:n], w[:, kc, c0:c0 + nco], x[:, kc, 0:n], start=(kc == 0), stop=(kc == KC - 1))

    def proj_tm(w, c0, nco, x, t0, ps):
        for kc in range(KC):
            P.mm(ps[:, 0:nco], x[:, kc, t0:t0 + 128], w[:, kc, c0:c0 + nco], start=(kc == 0), stop=(kc == KC - 1))

    def residual_update(ps_list_fn, n, t0, Gm, l, col, yT, nchunks_fn):
        pass

    def body():
        P.dma("sp", sm, V(smd, ["smd"]), "c_sm")
        P.dma("pool", cm, V(cmd.rearrange("p (a b) -> p a b", a=NMAT), ["cmd"]), "c_cm")
        P.dma("pool", V(w2.ap[0:64, :], w2.keys), V(w2d, ["w2d"]), "c_w2")
        P.memset("dve", onesf, 1.0)
        P.memset("dve", epsb, EPS)
        NPIECE = 8
        step = (WE + NPIECE - 1) // NPIECE
        for l in range(NL if "precast" not in SKIP else 0):
            for i in range(NPIECE):
                a, b = i * step, min(WE, (i + 1) * step)
                P.dma("pool", V(wbf[l, :, a:b], [("wbf", l, i)]), V(Wd[l, :, a:b], ["Wd"]), ("wbf", l, i))
        if "misc" in SKIP:
            return body2()
        t = tmp()
        tv = V(t.ap[:, 0:24].rearrange("p (a b) -> p a b", a=8), t.keys)
        P.act(tv, cTv, AF.Exp, scale=-1.0)
        P.ts("dve", tv, tv, 1.0, None, ALU.add)
        P.recip(tv, tv)
        P.tt("dve", scT, cTv, tv, ALU.mult)
        t2 = tmp()
        for j, sgn in ((0, 1.0), (1, -1.0)):
            P.tt("dve", smt, lp[:, :, 2 * j, :], lp[:, :, 2 * j + 1, :], ALU.mult)
            P.add("dve", lambda e, j=j: e.reduce_sum(out=t2.ap[:, j * 4:(j + 1) * 4], in_=smt.ap, axis=mybir.AxisListType.X),
                  [smt], [t2])
        P.act(V(t2.ap[:, 8:16], t2.keys), V(t2.ap[:, 0:8], t2.keys), AF.Exp)
        P.tt("dve", lamneg, V(t2.ap[:, 12:16], t2.keys), V(t2.ap[:, 8:12], t2.keys), ALU.subtract)
        for l in range(4):
            P.ts("dve", lamneg[:, l:l + 1], lamneg[:, l:l + 1], -LI[l], None, ALU.add)
            P.ts("dve", dgs[:, l:l + 1], dgn[:, l:l + 1], 1.0 - LI[l], None, ALU.mult)
        P.ts("dve", glagh, glag, 0.5, None, ALU.mult)
        body2()

    def body2():
        for s in range(NSEQ):
            P.barrier(barsc)
            P.dma("sp", V(h.ap[:, :, 0:256], [("h", 0)]), V(cxT[s], ["cxT"]), "hload")
            for b in range(1, NBLK):
                P.dma("sp", V(h.ap[:, :, b * NB:(b + 1) * NB], [("h", b)]),
                      V(xT[s][:, :, (b - 1) * NB:b * NB], ["xT"]), "hload")
            for l in range(NL):
                layer(s, l)
            P.barrier(barsc)
            for b in range(1, NBLK):
                P.dma("sp", V(outd[s][:, :, (b - 1) * NB:b * NB], [("out", s, b)]),
                      V(h.ap[:, :, b * NB:(b + 1) * NB], [("h", b)]), "out")
        P.barrier(barsc, full=True)

    def layer(s, l):
        need_ctx = l < DEPTH - 1
        colof = lambda blk: 2 if blk == 0 else s
        if STOP <= 0:
            return
        if s == 0:
            P.barrier(barsc)
            ps = PS[0]
            for g in range(12):
                w = wv(ws.get(l, "ada%d" % g), KC, 512)
                for j in range(4):
                    mc = g * 4 + j
                    for kc in range(KC):
                        P.mm(ps[:, mc * 3:mc * 3 + 3], w[:, kc, j * 128:(j + 1) * 128], scT[:, kc, :],
                             start=(kc == 0), stop=(kc == KC - 1))
            ps3 = V(ps.ap[:, 0:144].rearrange("p (m c) -> p m c", c=3), ps.keys)
            for c in range(3):
                P.tt("dve", modl[:, c, :], ps3[:, :, c], adab[:, l, :], ALU.add)
            for c in range(3):
                t = tmp()
                tv = V(t.ap[:, 0:8], t.keys)
                P.ts("dve", tv, modl[:, c, 8:16], 1.0, None, ALU.add)
                P.tt("dve", A1[:, l, c, :], tv, ng[:, l, 0, :], ALU.mult)
                P.copy("dve", B1[:, l, c, :], modl[:, c, 0:8])
                P.tt("dve", G1[:, l, c, :], modl[:, c, 16:24], ng[:, l, 1, :], ALU.mult)
                t = tmp()
                tv = V(t.ap[:, 0:8], t.keys)
                P.ts("dve", tv, modl[:, c, 32:40], 1.0, None, ALU.add)
                P.tt("dve", A2[:, l, c, :], tv, ng[:, l, 2, :], ALU.mult)
                P.copy("dve", B2[:, l, c, :], modl[:, c, 24:32])
                P.tt("dve", G2[:, l, c, :], modl[:, c, 40:48], ng[:, l, 3, :], ALU.mult)
        phase_begin(NB)
        ub = [M.alloc("ub%d" % i, [KC, NB], BF16) for i in range(2)]
        for blk in range(NBLK):
            dst = ub[blk % 2]
            norm_mod(blk * NB, NB, A1, B1, l, colof(blk), dst)
            P.dma("sp", V(ud[:, blk, :].rearrange("p (k t) -> p k t", k=KC), [("ud", blk)]),
                  V(dst.ap, [(dst.keys[0], k) for k in range(KC)]), "udw")
        if STOP <= 1:
            return
        for hp in range(2):
            gla_pair(s, l, hp)
        if STOP <= 2:
            return
        attn_c(s, l, need_ctx)
        if STOP <= 3:
            return
        attn_a_merge(s, l, need_ctx)
        if STOP <= 4:
            return
        ffn(s, l, need_ctx)

    def gla_pair(s, l, hp):
        phase_begin(NB, False, False)
        Q2 = M.alloc("Q2", [T], BF16)
        K2 = M.alloc("K2", [T], BF16)
        Ktm = M.alloc("Ktm", [18, 128], BF16)
        Vtm = M.alloc("Vtm", [18, 256], BF16)
        gb2 = M.alloc("gb2", [2, T], BF16)
        gtm = M.alloc("gtm", [18, 256], BF16)
        Sst = M.alloc("Sst", [18, 2, 2, 128], BF16)
        rTa = M.alloc("rTa", [NB], BF16)
        S2 = M.alloc("S2", [128], F32)
        S2b = M.alloc("S2b", [128], F32)
        Et = [M.alloc("Et%d" % i, [256], F32) for i in range(2)]
        Ent = [M.alloc("Ent%d" % i, [256], F32) for i in range(2)]
        ERt = [[M.alloc("ERt%d%d" % (d_, i), [128], F32) for i in range(2)] for d_ in range(2)]
        kend = [[M.alloc("kend%d%d" % (d_, i), [128], BF16) for i in range(2)] for d_ in range(2)]
        dec2 = [[M.alloc("dec%d%d" % (d_, i), [2], F32) for i in range(2)] for d_ in range(2)]
        qd = [M.alloc("qd%d" % i, [2, 128], BF16) for i in range(2)]
        kd = [M.alloc("kd%d" % i, [2, 128], BF16) for i in range(2)]
        attm = [M.alloc("attm%d" % i, [4, 128], BF16) for i in range(2)]
        obT = [M.alloc("obT%d" % i, [2, 128], F32) for i in range(2)]
        osq = [M.alloc("osq%d" % i, [2, 128], BF16) for i in range(2)]
        ybs = [M.alloc("ybs%d" % i, [4, NB], BF16) for i in range(1)]
        P.memset("pool", V(rTa.ap[32:64, :], rTa.keys), 1.0)
        for blk in range(NBLK):
            t0 = blk * NB
            u = load_u(blk)
            w1 = wv(ws.get(l, "g1_%d" % hp), KC, 288)
            ps = psr()
            proj_fm(w1, 0, 128, u, NB, ps)
            P.copy("act", V(Q2.ap[:, t0:t0 + NB], [("Q2", blk)]), ps[:, 0:NB])
            ps = psr()
            proj_fm(w1, 128, 128, u, NB, ps)
            P.copy("dve", V(K2.ap[:, t0:t0 + NB], [("K2", blk)]), ps[:, 0:NB])
            ps = psr()
            proj_fm(w1, 256, 32, u, NB, ps)
            P.copy("act", V(rTa.ap[0:32, :], rTa.keys), ps[0:32, 0:NB])
            for j in range(2):
                tb = blk * 2 + j
                ps = psr()
                proj_tm(w1, 128, 128, u, j * 128, ps)
                P.copy("dve", V(Ktm.ap[:, tb, :], [("Ktm", tb)]), ps[:, 0:128])
                ps = psr()
                P.mm(ps[:, 0:256], V(rTa.ap[0:64, j * 128:(j + 1) * 128], rTa.keys),
                     V(w2.ap[0:64, (l * 2 + hp) * 256:(l * 2 + hp + 1) * 256], w2.keys))
                t = tmp()
                tv = V(t.ap[:, 0:256], t.keys)
                P.act(tv, ps[:, 0:256], AF.Exp, scale=-1.0)
                t2 = tmp()
                tv2 = V(t2.ap[:, 0:256], t2.keys)
                P.act(tv2, tv, AF.Ln, bias=onesf[:, 0:1])
                P.ts("dve", V(gtm.ap[:, tb, :], [("gtm", tb)]), tv2, -1.0 / 16.0, None, ALU.mult)
            w2g = wv(ws.get(l, "g2_%d" % hp), KC, 512)
            for j in range(2):
                tb = blk * 2 + j
                ps = psr()
                proj_tm(w2g, 0, 256, u, j * 128, ps)
                P.copy("act", V(Vtm.ap[:, tb, :], [("Vtm", tb)]), ps[:, 0:256])
            for hh in range(2):
                ps = psr()
                proj_fm(w2g, 256 + hh * 128, 128, u, NB, ps)
                t = tmp()
                tv = V(t.ap[:, 0:NB], t.keys)
                P.act(tv, ps[:, 0:NB], AF.Exp, scale=-1.0)
                P.ts("dve", tv, tv, 1.0, None, ALU.add)
                P.recip(tv, tv)
                P.tt("dve", V(gb2.ap[:, hh, t0:t0 + NB], [("gb2", blk)]), tv, ps[:, 0:NB], ALU.mult)
        if STOP <= 1.3:
            return
        order_f = [(tb, c) for tb in range(18) for c in range(2)]
        order_b = [(tb, c) for tb in (1, 0) for c in (1, 0)] + [(tb, c) for tb in range(17, 1, -1) for c in (1, 0)]
        S2d = [S2, S2b]
        tbs_d = []
        for order in (order_f, order_b):
            tbs = []
            for (tb, c) in order:
                if not tbs or tbs[-1] != tb:
                    tbs.append(tb)
            tbs_d.append(tbs)
        for d in range(2):
            P.memset("dve", S2d[d], 0.0)

        def p1_local(d, tb):
            g_d = V(gtm.ap[:, tb, d * 128:(d + 1) * 128], [("gtm", tb)])
            psg = psr()
            P.mm(psg[:, 0:128], g_d, mat(6 + d))
            P.mm(psg[:, 128:256], mat(8 + d), g_d)
            er = ERt[d][tb % 2]
            P.act(er, psg[:, 128:256], AF.Exp)
            ke = kend[d][tb % 2]
            P.tt("dve", ke, V(Ktm.ap[:, tb, :], [("Ktm", tb)]), er, ALU.mult)
            dc = dec2[d][tb % 2]
            for cc in range(2):
                colx = cc * 64 + (63 if d == 0 else 0)
                P.act(dc[:, cc:cc + 1], psg[:, colx:colx + 1], AF.Exp)

        def p1_chain(d, tb, c):
            ke = kend[d][tb % 2]
            dc = dec2[d][tb % 2]
            S_ = S2d[d]
            P.copy("act", V(Sst.ap[:, tb, d, c, :], [("Sst", tb, d)]), S_)
            psu = psr()
            P.mm(psu[:, 0:256], V(ke.ap[c * 64:(c + 1) * 64, :], ke.keys),
                 V(Vtm.ap[c * 64:(c + 1) * 64, tb, :], [("Vtm", tb)]))
            for hh in range(2):
                P.stt("dve", V(S_.ap[hh * 64:(hh + 1) * 64, :], S_.keys), V(S_.ap[hh * 64:(hh + 1) * 64, :], S_.keys),
                      V(dc.ap[hh * 64:(hh + 1) * 64, c:c + 1], dc.keys),
                      psu[hh * 64:(hh + 1) * 64, hh * 128:(hh + 1) * 128], ALU.mult, ALU.add)

        for d in range(2):
            p1_local(d, tbs_d[d][0])
        for ti in range(18):
            for d in range(2):
                if ti + 1 < 18:
                    p1_local(d, tbs_d[d][ti + 1])
            for ci in range(2):
                for d in range(2):
                    c = ci if d == 0 else 1 - ci
                    p1_chain(d, tbs_d[d][ti], c)
        if STOP <= 1.6:
            return
        def p2_a(tb):
            blk = tb // 2
            t0 = tb * 128
            blk = tb // 2
            t0 = tb * 128
            E = Et[tb % 2]
            En = Ent[tb % 2]
            psg = psr()
            for d in range(2):
                P.mm(psg[:, d * 128:(d + 1) * 128], V(gtm.ap[:, tb, d * 128:(d + 1) * 128], [("gtm", tb)]), mat(6 + d))
            P.act(E, psg[:, 0:256], AF.Exp)
            P.act(En, psg[:, 0:256], AF.Exp, scale=-1.0)
            q_ = qd[tb % 2]
            k_ = kd[tb % 2]
            for d in range(2):
                P.tt("dve", q_[:, d, :], V(E.ap[:, d * 128:(d + 1) * 128], E.keys),
                     V(Q2.ap[:, t0:t0 + 128], [("Q2", blk)]), ALU.mult)
                P.tt("pool", k_[:, d, :], V(En.ap[:, d * 128:(d + 1) * 128], En.keys),
                     V(K2.ap[:, t0:t0 + 128], [("K2", blk)]), ALU.mult)

        def p2_b(tb):
            blk = tb // 2
            t0 = tb * 128
            q_ = qd[tb % 2]
            k_ = kd[tb % 2]
            psa2 = [psr(0, 4), psr(4, 8)]
            am = attm[tb % 2]
            for hh in range(2):
                for d in range(2):
                    P.mm(psa2[hh][:, d * 128:(d + 1) * 128], k_[hh * 64:(hh + 1) * 64, d, :], q_[hh * 64:(hh + 1) * 64, d, :])
            for hh in range(2):
                for d in range(2):
                    P.tt("dve", am[:, hh * 2 + d, :], mat(6 + d), psa2[hh][:, d * 128:(d + 1) * 128], ALU.mult)
            pso = psr()
            for hh in range(2):
                vt = V(Vtm.ap[:, tb, hh * 128:(hh + 1) * 128], [("Vtm", tb)])
                P.mm(pso[:, hh * 128:(hh + 1) * 128], vt, am[:, hh * 2, :], start=True, stop=False)
                P.mm(pso[:, hh * 128:(hh + 1) * 128], vt, am[:, hh * 2 + 1, :], start=False, stop=False)
                for c in range(2 if "sterm" not in SKIP else 0):
                    for d in range(2):
                        P.mm(pso[:, hh * 128 + c * 64:hh * 128 + (c + 1) * 64],
                             V(Sst.ap[hh * 64:(hh + 1) * 64, tb, d, c, :], [("Sst", tb, d)]),
                             q_[hh * 64:(hh + 1) * 64, d, c * 64:(c + 1) * 64],
                             start=False, stop=(c == 1 and d == 1))
            ob = obT[tb % 2]
            P.act(ob, V(pso.ap[:, 0:256].rearrange("p (a b) -> p a b", a=2), pso.keys), AF.Identity, scale=0.125)
            osq_ = osq[tb % 2]
            P.act(osq_, ob, AF.Square)
            psm = psr()
            P.mm(psm[:, 0:256], mat(2), V(osq_.ap.rearrange("p a b -> p (a b)"), osq_.keys))
            t = tmp()
            r = rstd_from(psm[:, 0:256], 256, out=V(t.ap[:, 0:256], t.keys))
            t2 = tmp()
            tv2 = V(t2.ap[:, 0:256].rearrange("p (a b) -> p a b", a=2), t2.keys)
            P.tt("dve", tv2, ob, V(r.ap.rearrange("p (a b) -> p a b", a=2), r.keys), ALU.mult)
            yb_ = ybs[0]
            j = tb % 2
            P.stt("dve", V(yb_.ap[:, 2 * hp:2 * hp + 2, j * 128:(j + 1) * 128], [(yb_.keys[0], j)]), tv2,
                  glag[:, l:l + 1], V(gb2.ap[:, :, t0:t0 + 128], [("gb2", blk)]), ALU.mult, ALU.mult)
            if j == 1 and "ydw" not in SKIP:
                P.dma("sp", V(yd[:, 1, blk, :].rearrange("p (k t) -> p k t", k=4)[:, 2 * hp:2 * hp + 2, :], [("yd", 1, blk, hp)]),
                      V(yb_.ap[:, 2 * hp:2 * hp + 2, :], [(yb_.keys[0], 0), (yb_.keys[0], 1)]), "ydw")


        p2_a(0)
        for tb in range(18):
            if tb + 1 < 18:
                p2_a(tb + 1)
            p2_b(tb)

    def rope_apply(src_bf, n, lt0, dst):
        psr_ = psr(5, 8)
        P.mm(psr_[:, 0:n], mat(5), src_bf)
        t1 = tmp()
        t1v = V(t1.ap[:, 0:n], t1.keys)
        P.tt("pool", t1v, src_bf, V(PH["rope"].ap[:, lt0:lt0 + n], PH["rope"].keys), ALU.mult)
        t2 = tmp()
        t2v = V(t2.ap[:, 0:n], t2.keys)
        P.tt("dve", t2v, psr_[:, 0:n], V(PH["rope"].ap[:, 2048 + lt0:2048 + lt0 + n], PH["rope"].keys), ALU.mult)
        P.tt("dve", dst, t1v, t2v, ALU.add)

    def attn_c(s, l, need_ctx):
        phase_begin(NB, True, False)
        kC = M.alloc("kC", [4, T], BF16)
        VC = M.alloc("VC", [18, 512], BF16)
        qr = M.alloc("qr", [4, NB], BF16)
        Es = [M.alloc("E%d" % i, [2, NB], BF16) for i in range(3)]
        raw = [M.alloc("raw%d" % i, [NB], BF16) for i in range(2)]
        rr = [M.alloc("rr%d" % i, [512], F32) for i in range(2)]
        oo = [M.alloc("oo%d" % i, [512], F32) for i in range(2)]
        oc = [M.alloc("oc%d" % i, [NB], F32) for i in range(2)]
        ocs = [M.alloc("ocs%d" % i, [NB], BF16) for i in range(2)]
        ycb = [M.alloc("ycb%d" % i, [4, NB], BF16) for i in range(2)]
        sacc = [[M.alloc("sacc%d%d" % (e_, m_), [2, NB], BF16) for m_ in range(2)] for e_ in range(2)]
        for blk in range(NBLK):
            t0 = blk * NB
            u = load_u(blk)
            w = wv(ws.get(l, "kvc1"), KC, 512)
            for c in range(4):
                ps = psr(0, 5)
                proj_fm(w, c * 128, 128, u, NB, ps)
                dst = V(kC.ap[:, c, t0:t0 + NB], [("kC", blk)])
                if blk == 0:
                    P.copy("act", dst, ps[:, 0:NB])
                else:
                    rw = raw[c % 2]
                    P.copy("act", rw, ps[:, 0:NB])
                    rope_apply(rw, NB, t0 - 256, dst)
            w = wv(ws.get(l, "kvc2"), KC, 512)
            for j in range(2):
                tb = blk * 2 + j
                ps = psr(0, 5)
                proj_tm(w, 0, 512, u, j * 128, ps)
                P.copy("act" if j == 0 else "dve", V(VC.ap[:, tb, :], [("VC", tb)]), ps[:, 0:512])
        P.barrier(barsc)
        for blk in range(NBLK):
            if blk == 0 and not need_ctx:
                continue
            t0 = blk * NB
            keys = [0, 1] if blk == 0 else list(range(18))
            u = load_u(blk)
            w = wv(ws.get(l, "qc"), KC, 512)
            for c in range(4):
                ps = PS[7]
                proj_fm(w, c * 128, 128, u, NB, ps)
                dst = V(qr.ap[:, c, :], [("qr", c)])
                if blk == 0:
                    P.copy("act", dst, ps[:, 0:NB])
                else:
                    rw = raw[c % 2]
                    P.copy("act", rw, ps[:, 0:NB])
                    rope_apply(rw, NB, t0 - 256, dst)
            yc_ = ycb[blk % 2]
            nkp = len(keys) // 2
            units = [(hd, m, kpi) for hd in range(4) for m in range(2) for kpi in range(nkp)]
            st = {}

            def c_s1(i):
                hd, m, kpi = units[i]
                off = m * 64
                pss = PS[i % 3]
                qv = V(qr.ap[off:off + 64, hd, :], [("qr", hd)])
                for j in range(2):
                    tb = keys[2 * kpi + j]
                    P.mm(pss[:, j * NB:(j + 1) * NB],
                         V(kC.ap[off:off + 64, hd, tb * 128:(tb + 1) * 128], [("kC", tb // 2)]), qv)
                E = Es[i % 3]
                P.act(E, V(pss.ap.rearrange("p (a b) -> p a b", a=2), pss.keys), AF.Exp, scale=0.125)

            def c_s3(i):
                hd, m, kpi = units[i]
                E = Es[i % 3]
                acc_o = PS[3 + hd % 2]
                acc_s = PS[5 + hd % 2]
                for j in range(2):
                    tb = keys[2 * kpi + j]
                    first = (kpi == 0 and j == 0)
                    last = (kpi == nkp - 1 and j == 1)
                    P.mm(acc_o[:, m * NB:(m + 1) * NB], V(VC.ap[:, tb, hd * 128:(hd + 1) * 128], [("VC", tb)]),
                         E[:, j, :], start=first, stop=last)
                eng = "dve" if kpi % 2 == 0 else "pool"
                sa = sacc[kpi % 2][m]
                if kpi < 2:
                    P.copy(eng, sa, E)
                else:
                    P.tt(eng, sa, sa, E, ALU.add)
                if m == 1 and kpi == nkp - 1:
                    for m2 in range(2):
                        srcs = [sacc[0][m2]] + ([sacc[1][m2]] if nkp > 1 else [])
                        nmm = 2 * len(srcs)
                        k_ = 0
                        for sa2 in srcs:
                            for j in range(2):
                                P.mm(acc_s[:, m2 * NB:(m2 + 1) * NB], mat(4), sa2[:, j, :], start=(k_ == 0), stop=(k_ == nmm - 1))
                                k_ += 1
                    r_ = rr[hd % 2]
                    o_ = oo[hd % 2]
                    P.recip(r_, acc_s)
                    P.tt("dve", o_, acc_o, r_, ALU.mult)
                    oc_ = oc[hd % 2]
                    P.stt("dve", oc_, V(o_.ap[:, NB:2 * NB], o_.keys), lamneg[:, l:l + 1], V(o_.ap[:, 0:NB], o_.keys),
                          ALU.mult, ALU.add)
                    os_ = ocs[hd % 2]
                    P.tt("pool", os_, oc_, oc_, ALU.mult)
                    psm = PS[7]
                    P.mm(psm[:, 0:NB], mat(2), os_)
                    t = tmp()
                    r = rstd_from(psm[:, 0:NB], NB, out=V(t.ap[:, 0:NB], t.keys))
                    P.stt("dve", V(yc_.ap[:, hd, :], [(yc_.keys[0], hd)]), oc_, dgs[:, l:l + 1], r, ALU.mult, ALU.mult)

            for i in range(len(units)):
                c_s1(i)
                if i >= 2:
                    c_s3(i - 2)
            c_s3(len(units) - 2)
            c_s3(len(units) - 1)
            P.dma("sp", V(yd[:, 2, blk, :].rearrange("p (k t) -> p k t", k=4), [("yd", 2, blk)]),
                  V(yc_.ap, [(yc_.keys[0], i) for i in range(4)]), "ydw")

    def attn_a_merge(s, l, need_ctx):
        phase_begin(NB, True)
        kA = M.alloc("kA", [T], BF16)
        VA = M.alloc("VA", [18, 2, 128], BF16)
        qr = M.alloc("qr", [4, NB], BF16)
        Es = [M.alloc("E%d" % i, [2, NB], BF16) for i in range(3)]
        kn = [M.alloc("kn%d" % i, [NB], BF16) for i in range(2)]
        ksq = [M.alloc("ksq%d" % i, [NB], BF16) for i in range(2)]
        rs = [M.alloc("rs%d" % i, [NB], F32) for i in range(2)]
        ya2 = [M.alloc("ya%d" % i, [4, NB], BF16) for i in range(2)]
        ybc2 = [M.alloc("ybc%d" % i, [2, 4, NB], BF16) for i in range(2)]
        pending = []
        mT = M.alloc("mT", [KC, NB], BF16)
        yT = M.alloc("yT", [KC, NB], F32)
        sg = [M.alloc("sg%d" % i, [NB], F32) for i in range(3)]
        macc = [M.alloc("macc%d" % i, [NB], F32) for i in range(2)]
        PH["mT"], PH["yT"], PH["sg"], PH["macc"] = mT, yT, sg, macc
        P.memset("pool", V(VA.ap[:, :, :, 64:128], [("VA", tb) for tb in range(18)]), 1.0)

        def qknorm(ps, which, is_lat, lt0, dst):
            i = _tc[0] % 2
            P.act(ksq[i], ps[:, 0:NB], AF.Square)
            psm = psr(5, 8)
            P.mm(psm[:, 0:NB], mat(3), ksq[i])
            t = tmp()
            r = rstd_from(psm[:, 0:NB], NB, out=V(t.ap[:, 0:NB], t.keys))
            if is_lat:
                P.stt("dve", kn[i], ps[:, 0:NB], qkg[:, l, which:which + 1], r, ALU.mult, ALU.mult)
                rope_apply(kn[i], NB, lt0, dst)
            else:
                P.stt("dve", dst, ps[:, 0:NB], qkg[:, l, which:which + 1], r, ALU.mult, ALU.mult)

        for blk in range(NBLK):
            t0 = blk * NB
            u = load_u(blk)
            w = wv(ws.get(l, "kva"), KC, 256)
            ps = psr(0, 5)
            proj_fm(w, 0, 128, u, NB, ps)
            qknorm(ps, 1, blk > 0, t0 - 256, V(kA.ap[:, t0:t0 + NB], [("kA", blk)]))
            for j in range(2):
                tb = blk * 2 + j
                ps = psr(0, 5)
                proj_tm(w, 128, 128, u, j * 128, ps)
                P.copy("act", V(VA.ap[:, tb, :, 0:64], [("VA", tb)]),
                       V(ps.ap[:, 0:128].rearrange("p (a b) -> p a b", a=2), ps.keys))
        P.barrier(barsc)
        for blk in range(NBLK):
            if blk == 0 and not need_ctx:
                continue
            t0 = blk * NB
            col = 2 if blk == 0 else s
            keys = [0, 1] if blk == 0 else list(range(18))
            u = load_u(blk)
            ya = ya2[blk % 2]
            ybc = ybc2[blk % 2]
            P.dma("sp", ybc, V(yd[:, 1:3, blk, :].rearrange("p a (k t) -> p a k t", k=4), ["ydr"]), ("ybc", blk % 2))
            w = wv(ws.get(l, "qa"), KC, 512)
            for c in range(4):
                ps = psr(5, 8)
                proj_fm(w, c * 128, 128, u, NB, ps)
                qknorm(ps, 0, blk > 0, t0 - 256, V(qr.ap[:, c, :], [("qr", c)]))
            nkp = len(keys) // 2
            units = [(c, kv, kpi) for c in range(4) for kv in range(2) for kpi in range(nkp)]

            def a_s1(i):
                c, kv, kpi = units[i]
                off = kv * 64
                pss = PS[i % 3]
                qv = V(qr.ap[off:off + 64, c, :], [("qr", c)])
                for j in range(2):
                    tb = keys[2 * kpi + j]
                    P.mm(pss[:, j * NB:(j + 1) * NB],
                         V(kA.ap[off:off + 64, tb * 128:(tb + 1) * 128], [("kA", tb // 2)]), qv)
                E = Es[i % 3]
                P.act(E, V(pss.ap.rearrange("p (a b) -> p a b", a=2), pss.keys), AF.Exp, scale=0.125)

            def a_s3(i):
                c, kv, kpi = units[i]
                E = Es[i % 3]
                acc = PS[3 + c % 2]
                for j in range(2):
                    tb = keys[2 * kpi + j]
                    P.mm(acc[:, kv * NB:(kv + 1) * NB], V(VA.ap[:, tb, kv, :], [("VA", tb)]), E[:, j, :],
                         start=(kpi == 0 and j == 0), stop=(kpi == nkp - 1 and j == 1))
                if kv == 1 and kpi == nkp - 1:
                    for kv2 in range(2):
                        r_ = rs[kv2]
                        P.recip(V(r_.ap[64:128, :], r_.keys), acc[64:128, kv2 * NB:(kv2 + 1) * NB])
                        po = (c % 2) * 64
                        P.tt("dve", V(ya.ap[po:po + 64, kv2 * 2 + c // 2, :], [(ya.keys[0], kv2 * 2 + c // 2, po)]),
                             acc[0:64, kv2 * NB:(kv2 + 1) * NB], V(r_.ap[64:128, :], r_.keys), ALU.mult)

            nun = len(units)
            nst = len(pending)
            done = 0
            for i in range(nun):
                a_s1(i)
                if i >= 2:
                    a_s3(i - 2)
                want = (i + 1) * nst // nun
                while done < want:
                    pending[done]()
                    done += 1
            a_s3(nun - 2)
            a_s3(nun - 1)
            while done < nst:
                pending[done]()
                done += 1
            pending = merge_steps(l, col, t0, u, ya, ybc)
        for st_ in pending:
            st_()

    def merge_steps(l, col, t0, u, ya, ybc):
        mT, yT, sg, macc = PH["mT"], PH["yT"], PH["sg"], PH["macc"]
        ysrc = [lambda kc: V(ya.ap[:, kc, :], [(ya.keys[0], kc, 0), (ya.keys[0], kc, 64)]),
                lambda kc: ybc[:, 0, kc, :], lambda kc: ybc[:, 1, kc, :]]
        st = {}
        steps = []

        def mstep(m, n):
            if n == 0:
                slot = ws.get(l, "mg%d" % m)
                st["wb"] = V(slot.ap[:, 0:1536].rearrange("p (n k c) -> p n k c", n=3, k=4), slot.keys)
                st["wg"] = V(slot.ap[:, 1536:4608].rearrange("p (n k c) -> p n k c", n=3, k=8), slot.keys)
            wb, wg = st["wb"], st["wg"]
            ma = macc[m % 2]
            psg = psr(5, 8)
            for kc in range(KC):
                P.mm(psg[:, 0:NB], wg[:, n, kc, :], u[:, kc, 0:NB], start=(kc == 0), stop=(kc == KC - 1))
            psb = psr(5, 8)
            for kc in range(4):
                P.mm(psb[:, 0:NB], wb[:, n, kc, :], ysrc[n](kc), start=(kc == 0), stop=(kc == 3))
            g_ = sg[n]
            P.act(g_, psg[:, 0:NB], AF.Exp, scale=-1.0)
            P.act(g_, g_, AF.Ln, bias=onesf[:, 0:1])
            P.act(g_, g_, AF.Exp, scale=-1.0)
            if n == 0:
                P.tt("dve", ma, g_, psb[:, 0:NB], ALU.mult)
            else:
                P.tt("dve", g_, g_, psb[:, 0:NB], ALU.mult)
                if n == 1:
                    P.tt("pool", ma, ma, g_, ALU.add)
                else:
                    P.tt("pool", V(mT.ap[:, m, :], [("mT", m)]), ma, g_, ALU.add)

        def wostep(mo):
            if mo % 4 == 0:
                st["wo"] = wv(ws.get(l, "wo%d" % (mo // 4)), KC, 512)
            wcur = st["wo"]
            ps = psr(5, 8)
            for kc in range(KC):
                P.mm(ps[:, 0:NB], wcur[:, kc, (mo % 4) * 128:(mo % 4 + 1) * 128], V(mT.ap[:, kc, :], [("mT", kc)]),
                     start=(kc == 0), stop=(kc == KC - 1))
            P.copy("act", V(yT.ap[:, mo, 0:NB], [("yT", mo)]), ps[:, 0:NB])
            P.act(V(PH["sq"].ap[:, mo, 0:NB], [("sq", mo)]), ps[:, 0:NB], AF.Square)

        def nstep():
            psm = psr(5, 8)
            for mo in range(8):
                P.mm(psm[:, 0:NB], mat(0), V(PH["sq"].ap[:, mo, 0:NB], [("sq", mo)]), start=(mo == 0), stop=(mo == 7))
            st["r"] = rstd_from(psm[:, 0:NB], NB)

        def rstep(mo):
            t = tmp()
            tv = V(t.ap[:, 0:NB], t.keys)
            P.stt("dve", tv, V(yT.ap[:, mo, 0:NB], [("yT", mo)]), G1[:, l, col, mo:mo + 1], st["r"], ALU.mult, ALU.mult)
            P.tt("pool", hk(mo, t0, NB), tv, hk(mo, t0, NB), ALU.add)

        for m in range(8):
            for n in range(3):
                steps.append(lambda m=m, n=n: mstep(m, n))
        for mo in range(8):
            steps.append(lambda mo=mo: wostep(mo))
        steps.append(nstep)
        for mo in range(8):
            steps.append(lambda mo=mo: rstep(mo))
        return steps

    def out_and_residual(l, col, t0, n, rhs_fn, nk, wnames, Gm, yT, matidx):
        for mo in range(8):
            if len(wnames) == 2:
                if mo % 4 == 0:
                    wcur = wv(ws.get(l, wnames[mo // 4]), KC, 512)
                lhs = lambda kc: wcur[:, kc, (mo % 4) * 128:(mo % 4 + 1) * 128]
            else:
                wcur = wv(ws.get(l, wnames[mo]), nk, 128)
                lhs = lambda kc: wcur[:, kc, :]
            ps = psr(0, 5)
            for kc in range(nk):
                P.mm(ps[:, 0:n], lhs(kc), rhs_fn(kc), start=(kc == 0), stop=(kc == nk - 1))
            P.copy("act", V(yT.ap[:, mo, 0:n], [("yT", mo)]), ps[:, 0:n])
            P.act(V(PH["sq"].ap[:, mo, 0:n], [("sq", mo)]), ps[:, 0:n], AF.Square)
        psm = psr(5, 8)
        for mo in range(8):
            P.mm(psm[:, 0:n], mat(0), V(PH["sq"].ap[:, mo, 0:n], [("sq", mo)]), start=(mo == 0), stop=(mo == 7))
        r = rstd_from(psm[:, 0:n], n)
        for mo in range(8):
            t = tmp()
            tv = V(t.ap[:, 0:n], t.keys)
            P.stt("dve", tv, V(yT.ap[:, mo, 0:n], [("yT", mo)]), Gm[:, l, col, mo:mo + 1], r, ALU.mult, ALU.mult)
            P.tt("pool", hk(mo, t0, n), tv, hk(mo, t0, n), ALU.add)

    def ffn(s, l, need_ctx):
        phase_begin(512)
        uT = PH["uT"]
        mid = M.alloc("mid", [FC, 512], BF16)
        yT = M.alloc("yTf", [KC, 512], F32)
        gb = [M.alloc("gbuf%d" % i, [516], F32) for i in range(3)]
        cb = [M.alloc("cb%d" % i, [512], F32) for i in range(3)]
        eb = [M.alloc("eb%d" % i, [512], F32) for i in range(3)]
        vbf = [M.alloc("vbf%d" % i, [512], F32) for i in range(3)]
        hh_ = M.alloc("hh", [KC, 8], F32)
        vh = M.alloc("vh", [KC, 8], BF16)
        ghalo = M.alloc("ghalo", [FC, 8], F32)
        blocks = [(0, 256, 2)] if need_ctx else []
        blocks += [(256 + i * 512, 512, s) for i in range(4)]
        halo_tok = [256 + 511, 256 + 512, 256 + 1023, 256 + 1024, 256 + 1535, 256 + 1536]
        for i, tk in enumerate(halo_tok):
            P.copy("dve", hh_[:, :, i:i + 1], V(h.ap[:, :, tk:tk + 1], [("h", tk // NB)]))
        for k in range(KC):
            P.act(V(PH["sq"].ap[:, k, 0:6], [("sq", k)]), hh_[:, k, 0:6], AF.Square)
        ps = psr(5, 8)
        for k in range(KC):
            P.mm(ps[:, 0:6], mat(0), V(PH["sq"].ap[:, k, 0:6], [("sq", k)]), start=(k == 0), stop=(k == KC - 1))
        r = rstd_from(ps[:, 0:6], 6)
        for k in range(KC):
            t = tmp()
            tv = V(t.ap[:, 0:6], t.keys)
            P.stt("dve", tv, hh_[:, k, 0:6], A2[:, l, s, k:k + 1], r, ALU.mult, ALU.mult)
            P.act(vh[:, k, 0:6], tv, AF.Identity, bias=B2[:, l, s, k:k + 1])
        for g in range(11):
            w = wv(ws.get(l, "up%d" % g), KC, 512)
            for jj in range(2):
                ps = psr(0, 5)
                for kc in range(KC):
                    P.mm(ps[:, 0:6], w[:, kc, jj * 256:jj * 256 + 128], vh[:, kc, 0:6], start=(kc == 0), stop=(kc == KC - 1))
                P.copy("act", ghalo[:, 2 * g + jj, 0:6], ps[:, 0:6])
        for bi, (t0, n, col) in enumerate(blocks):
            norm_mod(t0, n, A2, B2, l, col, uT)
            li = bi - (1 if need_ctx else 0)
            wst = {}

            def f_s1(j):
                g, jj = j // 2, j % 2
                if jj == 0:
                    wst["w"] = wv(ws.get(l, "up%d" % g), KC, 512)
                w = wst["w"]
                psg = psr(0, 3)
                psv = psr(3, 5)
                for kc in range(KC):
                    P.mm(psg[:, 0:n], w[:, kc, jj * 256:jj * 256 + 128], V(uT.ap[:, kc, 0:n], [("uT", kc)]),
                         start=(kc == 0), stop=(kc == KC - 1))
                for kc in range(KC):
                    P.mm(psv[:, 0:n], w[:, kc, jj * 256 + 128:jj * 256 + 256], V(uT.ap[:, kc, 0:n], [("uT", kc)]),
                         start=(kc == 0), stop=(kc == KC - 1))
                gbuf = gb[j % 3]
                P.copy("act", gbuf[:, 1:n + 1], psg[:, 0:n])
                vb_ = vbf[j % 3]
                P.copy("act", V(vb_.ap[:, 0:n], vb_.keys), psv[:, 0:n])
                if col != 2 and li > 0:
                    P.copy("pool", gbuf[:, 0:1], ghalo[:, j, 2 * li - 2:2 * li - 1])
                else:
                    P.memset("pool", gbuf[:, 0:1], 0.0)
                if col != 2 and li < 3:
                    P.copy("pool", gbuf[:, n + 1:n + 2], ghalo[:, j, 2 * li + 1:2 * li + 2])
                else:
                    P.memset("pool", gbuf[:, n + 1:n + 2], 0.0)
                c_ = cb[j % 3]
                cv = V(c_.ap[:, 0:n], c_.keys)
                P.ts("dve", cv, gbuf[:, 1:n + 1], convw[:, l, j, 1:2], convb[:, l, j:j + 1], ALU.mult, ALU.add)
                P.stt("dve", cv, gbuf[:, 0:n], convw[:, l, j, 0:1], cv, ALU.mult, ALU.add)
                P.stt("dve", cv, gbuf[:, 2:n + 2], convw[:, l, j, 2:3], cv, ALU.mult, ALU.add)

            def f_s2(j):
                c_ = cb[j % 3]
                cv = V(c_.ap[:, 0:n], c_.keys)
                vb_ = vbf[j % 3]
                vb = V(vb_.ap[:, 0:n], vb_.keys)
                e_ = eb[j % 3]
                ev = V(e_.ap[:, 0:n], e_.keys)
                P.act(ev, cv, AF.Exp, scale=-1.0)
                P.act(ev, ev, AF.Ln, bias=onesf[:, 0:1])
                P.act(ev, ev, AF.Exp, scale=-1.0)
                P.tt("pool", ev, ev, cv, ALU.mult)
                P.tt("pool", V(mid.ap[:, j, 0:n], [("mid", j)]), ev, vb, ALU.mult)

            f_s1(0)
            for j in range(FC):
                if j + 1 < FC:
                    f_s1(j + 1)
                f_s2(j)
            out_and_residual(l, col, t0, n, lambda kc: V(mid.ap[:, kc, 0:n], [("mid", kc)]), FC,
                             tuple("dn%d" % m for m in range(8)), G2, yT, 0)

    P.dry = True
    body()
    P.dry = False
    P.ops = []
    _tc[0] = 0
    _pc[0] = 0
    body()
    P.emit(nc)
    return nc


_CACHE = {}


def kernel(**inputs):
    inp = {k: np.asarray(v) for k, v in inputs.items()}
    B = inp["x"].shape[0]
    ncore = 8
    per = B // ncore
    sm, cmat, rope, w2 = prep_consts(inp)
    W = np.stack([prep_layer(inp, l) for l in range(DEPTH)], axis=0)
    if "nc" not in _CACHE:
        _CACHE["nc"] = build(DEPTH, per)
    nc = _CACHE["nc"]
    in_maps = []
    for c in range(ncore):
        xs = inp["x"][c * per:(c + 1) * per]
        xT = np.ascontiguousarray(xs.reshape(per, 2048, KC, 128).transpose(0, 3, 2, 1))
        cs = inp["ctx"][c * per:(c + 1) * per]
        cxT = np.ascontiguousarray(cs.reshape(per, 256, KC, 128).transpose(0, 3, 2, 1))
        smc = sm.copy()
        cc = np.concatenate([inp["c"][c * per:(c + 1) * per], inp["c_ctx"][None, :]], axis=0)
        if per == 1:
            cc = np.concatenate([cc[0:1], cc[0:1], cc[1:2]], axis=0)
        o, e = SM["cT"]
        smc[:, o:o + e] = cc.reshape(3, KC, 128).transpose(2, 1, 0).reshape(128, e)
        in_maps.append({"xT": xT, "cxT": cxT, "smallc": smc, "cmat": cmat, "rope": rope, "w2aug": w2, "W": W})
    res = run_bass_kernel_spmd(nc, in_maps, core_ids=list(range(ncore)))
    outs = []
    for c in range(ncore):
        o = np.asarray(res.results[c]["out"])
        outs.append(o.transpose(0, 3, 2, 1).reshape(per, 2048, D))
    return np.concatenate(outs, axis=0).astype(np.float32)
```

```python
import math
import os
STOP = float(os.environ.get("KSTOP", "9"))
SKIP = os.environ.get("KSKIP", "").split(",")
from contextlib import ExitStack
import numpy as np
import concourse.bass as bass
import concourse.mybir as mybir
from concourse.bass_utils import run_bass_kernel_spmd

F32 = mybir.dt.float32
BF16 = mybir.dt.bfloat16
U8 = mybir.dt.uint8
AF = mybir.ActivationFunctionType
ALU = mybir.AluOpType

D = 1024
KC = 8
NB = 256
NBLK = 9
T = 2304
DEPTH = 4
EPS = 1e-6
FFN = 2816
FC = 22
LI = [0.8 - 0.6 * math.exp(-0.3 * l) for l in range(DEPTH)]


class V:
    def __init__(self, ap, keys):
        self.ap = ap
        self.keys = tuple(keys)

    def __getitem__(self, idx):
        return V(self.ap[idx], self.keys)

    def k(self, *keys):
        return V(self.ap, keys)


class Op:
    __slots__ = ("eng", "fn", "reads", "writes", "dma", "semkey", "barrier", "deps", "signal", "val", "waits", "full")


class Prog:
    ENGS = ("pe", "act", "dve", "pool", "sp")

    def __init__(self):
        self.ops = []
        self.dry = False

    def add(self, eng, fn, ins=(), outs=(), dma=False, semkey=None, barrier=False, full=False):
        if self.dry:
            return
        op = Op()
        op.full = full
        op.eng = eng
        op.fn = fn
        r = []
        for v in ins:
            if isinstance(v, V):
                r.extend(v.keys)
        w = []
        for v in outs:
            w.extend(v.keys)
        op.reads = r
        op.writes = w
        op.dma = dma
        op.semkey = semkey
        op.barrier = barrier
        self.ops.append(op)

    def mm(self, out, lhsT, rhs, start=True, stop=True):
        self.add("pe", lambda e: e.matmul(out.ap, lhsT.ap, rhs.ap, start=start, stop=stop), [lhsT, rhs], [out])

    def act(self, out, in_, func, scale=1.0, bias=None):
        b = bias.ap if isinstance(bias, V) else bias
        if b is None:
            self.add("act", lambda e: e.activation(out=out.ap, in_=in_.ap, func=func, scale=scale), [in_], [out])
        else:
            self.add("act", lambda e: e.activation(out=out.ap, in_=in_.ap, func=func, scale=scale, bias=b),
                     [in_, bias], [out])

    def ts(self, eng, out, in0, s1, s2, op0, op1=None):
        a1 = s1.ap if isinstance(s1, V) else s1
        a2 = s2.ap if isinstance(s2, V) else s2
        if op1 is None:
            f = lambda e: e.tensor_scalar(out=out.ap, in0=in0.ap, scalar1=a1, scalar2=None, op0=op0)
        else:
            f = lambda e: e.tensor_scalar(out=out.ap, in0=in0.ap, scalar1=a1, scalar2=a2, op0=op0, op1=op1)
        self.add(eng, f, [in0, s1, s2], [out])

    def stt(self, eng, out, in0, scalar, in1, op0, op1):
        a = scalar.ap if isinstance(scalar, V) else scalar
        self.add(eng, lambda e: e.scalar_tensor_tensor(out=out.ap, in0=in0.ap, scalar=a, in1=in1.ap, op0=op0, op1=op1),
                 [in0, scalar, in1], [out])

    def tt(self, eng, out, in0, in1, op):
        self.add(eng, lambda e: e.tensor_tensor(out=out.ap, in0=in0.ap, in1=in1.ap, op=op), [in0, in1], [out])

    def copy(self, eng, out, in_):
        if eng == "act":
            self.add("act", lambda e: e.copy(out=out.ap, in_=in_.ap), [in_], [out])
        else:
            self.add(eng, lambda e: e.tensor_copy(out=out.ap, in_=in_.ap), [in_], [out])

    def recip(self, out, in_):
        self.add("dve", lambda e: e.reciprocal(out=out.ap, in_=in_.ap), [in_], [out])

    def memset(self, eng, out, val):
        self.add(eng, lambda e: e.memset(out.ap, val), [], [out])

    def dma(self, eng, out, in_, semkey):
        self.add(eng, lambda e: e.dma_start(out=out.ap, in_=in_.ap), [in_], [out], dma=True, semkey=semkey)

    def barrier(self, scratch, full=False):
        self.add("pool", lambda e: e.memset(scratch.ap, 0.0), [], [scratch], barrier=True, full=full)

    def analyze(self):
        ops = self.ops
        lastw = {}
        readers = {}
        last_on = {}
        dma_cnt = {}
        sig_cnt = {e: 0 for e in self.ENGS}
        bar_idx = None
        synced = set()
        for i, op in enumerate(ops):
            deps = {}
            if op.barrier:
                for e, j in last_on.items():
                    deps[j] = "raw"
                op.deps = (deps, {k_: v_ for k_, v_ in dma_cnt.items() if op.full or not (isinstance(k_, tuple) and k_[0] == "wbf")})
                bar_idx = i
                synced = {op.eng}
            else:
                for r in op.reads:
                    j = lastw.get(r)
                    if j is not None:
                        deps[j] = "raw"
                for w in op.writes:
                    j = lastw.get(w)
                    if j is not None and j not in deps:
                        deps[j] = "waw"
                    for j2 in readers.get(w, ()):
                        if j2 not in deps:
                            deps[j2] = "war"
                if bar_idx is not None and op.eng not in synced:
                    deps[bar_idx] = "raw"
                    synced.add(op.eng)
                op.deps = (deps, None)
            for r in op.reads:
                readers.setdefault(r, []).append(i)
            for w in op.writes:
                lastw[w] = i
                readers[w] = []
            last_on[op.eng] = i
            op.signal = op.dma
            if op.dma:
                dma_cnt[op.semkey] = dma_cnt.get(op.semkey, 0) + 1
            op.waits = None
        for i, op in enumerate(ops):
            deps, _ = op.deps
            keep = []
            for j, kind in deps.items():
                pj = ops[j]
                if j == i:
                    continue
                if pj.eng == op.eng and not pj.dma:
                    if op.eng == "pe":
                        continue
                keep.append(j)
                if not pj.dma:
                    pj.signal = True
            op.deps = (keep, op.deps[1])
        cnt = {e: 0 for e in self.ENGS}
        dcnt = {}
        snap = []
        for i, op in enumerate(ops):
            snap.append(None)
            if op.dma:
                dcnt[op.semkey] = dcnt.get(op.semkey, 0) + 1
                op.val = None
            elif op.signal:
                cnt[op.eng] += 1
                op.val = cnt[op.eng]
            else:
                op.val = None
        dcnt = {}
        for i, op in enumerate(ops):
            keep, bar = op.deps
            waits = {}
            for j in keep:
                pj = ops[j]
                if pj.dma:
                    key = ("d", pj.semkey)
                    val = 16 * dcnt[pj.semkey]
                else:
                    key = ("e", pj.eng)
                    val = pj.val
                if waits.get(key, 0) < val:
                    waits[key] = val
            if bar is not None:
                for sk, c in bar.items():
                    key = ("d", sk)
                    if waits.get(key, 0) < 16 * c:
                        waits[key] = 16 * c
            op.waits = waits
            if op.dma:
                dcnt[op.semkey] = dcnt.get(op.semkey, 0) + 1
        self.semkeys = list(dcnt.keys())

    def emit(self, nc):
        self.analyze()
        ops = self.ops
        with ExitStack() as es:
            sems = {}
            for e in self.ENGS:
                sems[("e", e)] = es.enter_context(nc.semaphore("se_" + e))
            for n, sk in enumerate(self.semkeys):
                sems[("d", sk)] = es.enter_context(nc.semaphore("sd%d" % n))
            block = es.enter_context(nc.Block())
            by = {e: [op for op in ops if op.eng == e] for e in self.ENGS}

            def run(engname, e):
                waited = {}
                for op in by[engname]:
                    for key, val in op.waits.items():
                        if waited.get(key, 0) >= val:
                            continue
                        waited[key] = val
                        e.wait_ge(sems[key], val)
                    ins = op.fn(e)
                    if op.dma:
                        ins.then_inc(sems[("d", op.semkey)], 16)
                    elif op.signal:
                        ins.then_inc(sems[("e", engname)], 1)

            @block.tensor
            def _(e):
                run("pe", e)

            @block.scalar
            def _(e):
                run("act", e)

            @block.vector
            def _(e):
                run("dve", e)

            @block.gpsimd
            def _(e):
                run("pool", e)

            @block.sync
            def _(e):
                run("sp", e)


def _kcl(W):
    K, C = W.shape
    return np.ascontiguousarray(W.reshape(K // 128, 128, C).transpose(1, 0, 2)).reshape(128, -1)


def weight_groups():
    g = []
    for i in range(12):
        g.append(("ada%d" % i, 8 * 512))
    for hp in range(2):
        g.append(("g1_%d" % hp, 8 * 288))
        g.append(("g2_%d" % hp, 8 * 512))
    g += [("kvc1", 8 * 512), ("kvc2", 8 * 512), ("qc", 8 * 512), ("kva", 8 * 256), ("qa", 8 * 512)]
    for m in range(8):
        g.append(("mg%d" % m, 1536 + 3072))
    g += [("wo0", 8 * 512), ("wo1", 8 * 512)]
    for i in range(11):
        g.append(("up%d" % i, 8 * 512))
    for m in range(8):
        g.append(("dn%d" % m, 22 * 128))
    return g


WG = weight_groups()
WOFF = {}
_o = 0
for _n, _e in WG:
    WOFF[_n] = (_o, _e)
    _o += _e
WE = _o
SLOT = 4608


def prep_layer(inp, l):
    w_in = inp["w_in"][l]
    pieces = []
    ada = inp["ada_w"][l]
    for i in range(12):
        pieces.append(_kcl(ada[:, i * 512:(i + 1) * 512]))
    for hp in range(2):
        cols = np.concatenate([np.arange(768 + hp * 128, 768 + hp * 128 + 128),
                               np.arange(1024 + hp * 128, 1024 + hp * 128 + 128),
                               np.arange(2304, 2336)])
        pieces.append(_kcl(w_in[:, cols]))
        cols = np.concatenate([np.arange(1280 + hp * 256, 1280 + hp * 256 + 256),
                               np.arange(1792 + hp * 256, 1792 + hp * 256 + 256)])
        pieces.append(_kcl(w_in[:, cols]))
    pieces.append(_kcl(w_in[:, 2848:3360]))
    pieces.append(_kcl(w_in[:, 3360:3872]))
    pieces.append(_kcl(w_in[:, 2336:2848]))
    pieces.append(_kcl(w_in[:, 512:768]))
    cols = []
    for c in range(4):
        cols += list(range(c * 64, c * 64 + 64)) + list(range((4 + c) * 64, (4 + c) * 64 + 64))
    pieces.append(_kcl(w_in[:, np.array(cols)]))
    wbr = inp["w_branch"][l]
    for m in range(8):
        a = wbr[:, :, m * 128:(m + 1) * 128].reshape(3, 4, 128, 128).transpose(2, 0, 1, 3).reshape(128, -1)
        gcols = np.concatenate([np.arange(3872 + n * 1024 + m * 128, 3872 + n * 1024 + m * 128 + 128) for n in range(3)])
        b = w_in[:, gcols].reshape(8, 128, 3, 128).transpose(1, 2, 0, 3).reshape(128, -1)
        pieces.append(np.concatenate([a, b], axis=1))
    wo = inp["w_out"][l]
    pieces.append(_kcl(wo[:, 0:512]))
    pieces.append(_kcl(wo[:, 512:1024]))
    wu = inp["w_ffn_in"][l]
    for i in range(11):
        cols = np.concatenate([np.arange(j * 128, j * 128 + 128) if t == 0 else np.arange(FFN + j * 128, FFN + j * 128 + 128)
                               for j in (2 * i, 2 * i + 1) for t in (0, 1)])
        pieces.append(_kcl(wu[:, cols]))
    wd = inp["w_ffn_out"][l]
    for m in range(8):
        pieces.append(_kcl(wd[:, m * 128:(m + 1) * 128]))
    W = np.concatenate(pieces, axis=1)
    assert W.shape == (128, WE), W.shape
    return np.ascontiguousarray(W, dtype=np.float32)


SM = {}
_o = 0
for _n, _e in [("adab", 4 * 48), ("ng", 4 * 4 * 8), ("qkg", 4 * 2), ("glag", 4), ("dgn", 4), ("convw", 4 * 22 * 3),
               ("convb", 4 * 22), ("lp", 4 * 4 * 64), ("cT", 8 * 3)]:
    SM[_n] = (_o, _e)
    _o += _e
NSM = _o
NMAT = 14


def prep_consts(inp):
    sm = np.zeros((128, NSM), np.float32)

    def put(name, arr):
        o, e = SM[name]
        sm[:, o:o + e] = arr.reshape(128, e)

    put("adab", inp["ada_b"].reshape(4, 48, 128).transpose(2, 0, 1))
    put("ng", inp["norm_g"].reshape(4, 4, 8, 128).transpose(3, 0, 1, 2))
    put("qkg", np.tile(inp["qk_norm_a"].transpose(2, 0, 1), (2, 1, 1)))
    put("glag", inp["gla_norm"].T)
    put("dgn", inp["diff_norm"].T)
    put("convw", inp["ffn_conv_w"].reshape(4, 3, 22, 128).transpose(3, 0, 2, 1))
    put("convb", inp["ffn_conv_b"].reshape(4, 22, 128).transpose(2, 0, 1))
    put("lp", np.broadcast_to(inp["diff_lambda"].reshape(1, -1), (128, 1024)))
    cm = np.zeros((128, NMAT, 128), np.float32)
    cm[:, 0] = 1.0 / 1024
    cm[:, 1] = 0.25 / 1024
    cm[:, 2] = 1.0 / 128
    idx = np.arange(128)
    same64 = (idx[:, None] // 64) == (idx[None, :] // 64)
    cm[:, 3] = same64 / 64.0
    cm[:, 4] = 1.0
    P = np.zeros((128, 128), np.float32)
    for p in range(128):
        half = (p % 32) // 16
        if half == 0:
            P[p + 16, p] = -1.0
        else:
            P[p - 16, p] = 1.0
    cm[:, 5] = P
    l_ = idx[:, None]
    t_ = idx[None, :]
    cm[:, 6] = same64 & (l_ <= t_)
    cm[:, 7] = same64 & (l_ >= t_)
    cm[:, 8] = same64 & (l_ > t_)
    cm[:, 9] = same64 & (l_ < t_)
    cm[:, 10] = cm[:, 6]
    cm[:, 11] = cm[:, 7]
    cm[:, 12] = cm[:, 6]
    cm[:, 13] = cm[:, 7]
    d = idx % 64
    axis = d // 32
    f = d % 16
    inv = (10000.0 ** (-np.arange(16, dtype=np.float32) / 16)).astype(np.float32)
    t = np.arange(2048)
    row = (t // 64).astype(np.float32)
    col = (t % 64).astype(np.float32)
    pos = np.where(axis[:, None] == 0, row[None, :], col[None, :]).astype(np.float32)
    ang = (pos * inv[f][:, None]).astype(np.float32)
    rope = np.concatenate([np.cos(ang), np.sin(ang)], axis=1).astype(np.float32)
    w2 = np.zeros((64, 4, 2, 2, 2, 64), np.float32)
    wd = inp["gla_w_decay"].reshape(4, 2, 16, 4, 64)
    bd = inp["gla_b_decay"].reshape(4, 2, 4, 64)
    for dr in range(2):
        for hp in range(2):
            for hh in range(2):
                w2[dr * 16:(dr + 1) * 16, :, hp, dr, hh, :] = wd[:, dr, :, 2 * hp + hh, :].transpose(1, 0, 2)
                w2[32, :, hp, dr, hh, :] = bd[:, dr, 2 * hp + hh, :]
    w2 = w2.reshape(64, 2048)
    return sm, cm.reshape(128, NMAT * 128), rope, w2


class Mem:
    def __init__(self, big, cap):
        self.big = big
        self.cap = cap
        self.off = 0

    def alloc(self, name, shape, dtype, nparts=128):
        esz = 4 if dtype == F32 else 2
        n = 1
        for s in shape:
            n *= s
        nb = (n * esz + 31) // 32 * 32
        assert self.off + nb <= self.cap, ("SBUF overflow", name, self.off, nb, self.cap)
        ap = self.big[0:nparts, self.off:self.off + n * esz].bitcast(dtype)
        if len(shape) == 2:
            ap = ap.rearrange("p (a b) -> p a b", a=shape[0])
        elif len(shape) == 3:
            ap = ap.rearrange("p (a b c) -> p a b c", a=shape[0], b=shape[1])
        elif len(shape) == 4:
            ap = ap.rearrange("p (a b c d) -> p a b c d", a=shape[0], b=shape[1], c=shape[2])
        self.off += nb
        return V(ap, [name])


def build(NL=DEPTH, NSEQ=2):
    nc = bass.Bass("TRN2", target_bir_lowering=False)
    xT = nc.dram_tensor("xT", [NSEQ, 128, KC, 2048], F32, kind="ExternalInput").ap()
    cxT = nc.dram_tensor("cxT", [NSEQ, 128, KC, 256], F32, kind="ExternalInput").ap()
    smd = nc.dram_tensor("smallc", [128, NSM], F32, kind="ExternalInput").ap()
    cmd = nc.dram_tensor("cmat", [128, NMAT * 128], F32, kind="ExternalInput").ap()
    roped = nc.dram_tensor("rope", [128, 4096], F32, kind="ExternalInput").ap()
    w2d = nc.dram_tensor("w2aug", [64, 2048], F32, kind="ExternalInput").ap()
    Wd = nc.dram_tensor("W", [NL, 128, WE], F32, kind="ExternalInput").ap()
    outd = nc.dram_tensor("out", [NSEQ, 128, KC, 2048], F32, kind="ExternalOutput").ap()
    wbf = nc.dram_tensor("wbf", [NL, 128, WE], BF16, kind="Internal").ap()
    ud = nc.dram_tensor("ud", [128, NBLK, KC * NB], BF16, kind="Internal").ap()
    yd = nc.dram_tensor("yd", [128, 3, NBLK, 4 * NB], BF16, kind="Internal").ap()

    cap = (nc.sbuf_bytes_remaining - 1024) // 64 * 64
    LIM = cap - int(os.environ.get("KTOP", "0"))
    big = nc.alloc_sbuf_tensor("big", [128, cap], U8)
    M = Mem(big, LIM)
    PS = [V(nc.alloc_psum_tensor("ps%d" % i, [128, 512], F32)[:, :], [("ps", i)]) for i in range(8)]
    P = Prog()

    sm = M.alloc("sm", [NSM], F32)
    cm = M.alloc("cm", [NMAT, 128], BF16)
    w2 = M.alloc("w2", [2048], BF16)
    onesf = M.alloc("onesf", [128], F32)

    def smv(name, *shape):
        o, e = SM[name]
        ap = sm.ap[:, o:o + e]
        if len(shape) == 2:
            ap = ap.rearrange("p (a b) -> p a b", a=shape[0])
        elif len(shape) == 3:
            ap = ap.rearrange("p (a b c) -> p a b c", a=shape[0], b=shape[1])
        return V(ap, ["sm"])

    adab = smv("adab", 4, 48)
    ng = smv("ng", 4, 4, 8)
    qkg = smv("qkg", 4, 2)
    glag = smv("glag")
    dgn = smv("dgn")
    convw = smv("convw", 4, 22, 3)
    convb = smv("convb", 4, 22)
    lp = smv("lp", 4, 4, 64)
    cTv = smv("cT", 8, 3)
    scT = M.alloc("scT", [8, 3], BF16)
    modl = M.alloc("modl", [3, 48], F32)
    A1 = M.alloc("A1", [4, 3, 8], F32)
    B1 = M.alloc("B1", [4, 3, 8], F32)
    G1 = M.alloc("G1", [4, 3, 8], F32)
    A2 = M.alloc("A2", [4, 3, 8], F32)
    B2 = M.alloc("B2", [4, 3, 8], F32)
    G2 = M.alloc("G2", [4, 3, 8], F32)
    lamneg = M.alloc("lamneg", [4], F32)
    dgs = M.alloc("dgs", [4], F32)
    glagh = M.alloc("glagh", [4], F32)
    smt = M.alloc("smt", [4, 64], F32)
    barsc = M.alloc("barsc", [8], F32)
    h = M.alloc("h", [KC, T], F32)
    NSLOT = 2
    wsl = [M.alloc("wsl%d" % i, [SLOT], BF16) for i in range(NSLOT)]
    rstd = M.alloc("rstd", [512], F32)
    tmps = [M.alloc("tmp%d" % i, [512], F32) for i in range(4)]
    epsb = M.alloc("epsb", [1], F32)
    persist_end = M.off
    PH = {}

    def phase_begin(ntok, need_rope=False, need_sq=True):
        P.barrier(barsc)
        M.off = persist_end
        PH["uT"] = M.alloc("uT", [KC, ntok], BF16)
        PH.pop("uT2", None)
        if ntok == NB:
            PH["uT2"] = M.alloc("uT2", [KC, ntok], BF16)
        if need_sq:
            PH["sq"] = M.alloc("sq", [KC, ntok], BF16)
        if need_rope:
            PH["rope"] = M.alloc("rope", [4096], F32)
            P.dma("sp", PH["rope"], V(roped, ["roped"]), "c_rope")
    _tc = [0]

    def tmp():
        _tc[0] += 1
        return tmps[_tc[0] % 4]

    _pc = [0]

    def psr(lo=0, hi=8):
        _pc[0] += 1
        return PS[lo + _pc[0] % (hi - lo)]

    def hk(k, t0, n):
        keys = [("h", b) for b in range(t0 // NB, (t0 + n - 1) // NB + 1)]
        return V(h.ap[:, k, t0:t0 + n], keys)

    def mat(i):
        return V(cm.ap[:, i, :], ["cm"])

    class WS:
        def __init__(self):
            self.plan = []
            self.i = 0
            self.issued = 0

        def get(self, l, name):
            if P.dry:
                self.plan.append((l, name))
                return wsl[0]
            assert self.plan[self.i] == (l, name), (self.plan[self.i], l, name)
            while self.issued < min(len(self.plan), self.i + NSLOT):
                ll, nn = self.plan[self.issued]
                o, e = WOFF[nn]
                s = wsl[self.issued % NSLOT]
                pstep = (WE + 7) // 8
                pk = [("wbf", ll, i) for i in range(o // pstep, (o + e - 1) // pstep + 1)]
                P.dma("sp", V(s.ap[:, 0:e], s.keys), V(wbf[ll, :, o:o + e], pk), ("wsl", self.issued % NSLOT))
                self.issued += 1
            s = wsl[self.i % NSLOT]
            self.i += 1
            return s

    ws = WS()

    def wv(slot, kc, cols):
        return V(slot.ap[:, 0:kc * cols].rearrange("p (k c) -> p k c", k=kc), slot.keys)

    def rstd_from(ps_ms, n, out=None):
        o = out if out is not None else V(rstd.ap[:, 0:n], rstd.keys)
        t = tmp()
        tv = V(t.ap[:, 0:n], t.keys)
        P.act(tv, ps_ms, AF.Ln, bias=epsb)
        P.act(o, tv, AF.Exp, scale=-0.5)
        return o

    def norm_mod(t0, n, Am, Bm, l, col, dst):
        for k in range(KC):
            P.act(V(PH["sq"].ap[:, k, 0:n], [("sq", k)]), hk(k, t0, n), AF.Square)
        ps = psr(5, 8)
        for k in range(KC):
            P.mm(ps[:, 0:n], mat(0), V(PH["sq"].ap[:, k, 0:n], [("sq", k)]), start=(k == 0), stop=(k == KC - 1))
        r = rstd_from(ps[:, 0:n], n)
        for k in range(KC):
            t = tmp()
            tv = V(t.ap[:, 0:n], t.keys)
            P.stt("dve", tv, hk(k, t0, n), Am[:, l, col, k:k + 1], r, ALU.mult, ALU.mult)
            P.act(V(dst.ap[:, k, 0:n], [(dst.keys[0], k)]), tv, AF.Identity, bias=Bm[:, l, col, k:k + 1])

    def load_u(blk):
        PH["ui"] = PH.get("ui", 0) + 1
        uT = PH["uT"] if (PH["ui"] % 2 == 0 or "uT2" not in PH) else PH["uT2"]
        P.dma("sp", V(uT.ap[:, :, 0:NB], uT.keys), V(ud[:, blk, :].rearrange("p (k t) -> p k t", k=KC), ["ud"]),
              ("uT", uT.keys[0]))
        return uT

    def proj_fm(w, c0, nco, x, n, ps):
        for kc in range(KC):
            P.mm(ps[0:nco, 0:n], w[:, kc, c0:c0 + nco], x[:, kc, 0:n], start=(kc == 0), stop=(kc == KC - 1))

    def proj_tm(w, c0, nco, x, t0, ps):
        for kc in range(KC):
            P.mm(ps[:, 0:nco], x[:, kc, t0:t0 + 128], w[:, kc, c0:c0 + nco], start=(kc == 0), stop=(kc == KC - 1))

    def residual_update(ps_list_fn, n, t0, Gm, l, col, yT, nchunks_fn):
        pass

    def body():
        P.dma("sp", sm, V(smd, ["smd"]), "c_sm")
        P.dma("pool", cm, V(cmd.rearrange("p (a b) -> p a b", a=NMAT), ["cmd"]), "c_cm")
        P.dma("pool", V(w2.ap[0:64, :], w2.keys), V(w2d, ["w2d"]), "c_w2")
        P.memset("dve", onesf, 1.0)
        P.memset("dve", epsb, EPS)
        precast(0)
        if "misc" in SKIP:
            return body2()
        t = tmp()
        tv = V(t.ap[:, 0:24].rearrange("p (a b) -> p a b", a=8), t.keys)
        P.act(tv, cTv, AF.Exp, scale=-1.0)
        P.ts("dve", tv, tv, 1.0, None, ALU.add)
        P.recip(tv, tv)
        P.tt("dve", scT, cTv, tv, ALU.mult)
        t2 = tmp()
        for j, sgn in ((0, 1.0), (1, -1.0)):
            P.tt("dve", smt, lp[:, :, 2 * j, :], lp[:, :, 2 * j + 1, :], ALU.mult)
            P.add("dve", lambda e, j=j: e.reduce_sum(out=t2.ap[:, j * 4:(j + 1) * 4], in_=smt.ap, axis=mybir.AxisListType.X),
                  [smt], [t2])
        P.act(V(t2.ap[:, 8:16], t2.keys), V(t2.ap[:, 0:8], t2.keys), AF.Exp)
        P.tt("dve", lamneg, V(t2.ap[:, 12:16], t2.keys), V(t2.ap[:, 8:12], t2.keys), ALU.subtract)
        for l in range(4):
            P.ts("dve", lamneg[:, l:l + 1], lamneg[:, l:l + 1], -LI[l], None, ALU.add)
            P.ts("dve", dgs[:, l:l + 1], dgn[:, l:l + 1], 1.0 - LI[l], None, ALU.mult)
        P.ts("dve", glagh, glag, 0.5, None, ALU.mult)
        body2()

    def precast(l):
        NPIECE = 8
        step = (WE + NPIECE - 1) // NPIECE
        for i in range(NPIECE):
            a, b = i * step, min(WE, (i + 1) * step)
            P.dma("pool", V(wbf[l, :, a:b], [("wbf", l, i)]), V(Wd[l, :, a:b], ["Wd"]), ("wbf", l, i))

    def body2():
        for s in range(NSEQ):
            P.barrier(barsc)
            P.dma("sp", V(h.ap[:, :, 0:256], [("h", 0)]), V(cxT[s], ["cxT"]), "hload")
            for b in range(1, NBLK):
                P.dma("sp", V(h.ap[:, :, b * NB:(b + 1) * NB], [("h", b)]),
                      V(xT[s][:, :, (b - 1) * NB:b * NB], ["xT"]), "hload")
            for l in range(NL):
                layer(s, l)
            P.barrier(barsc)
            for b in range(1, NBLK):
                P.dma("sp", V(outd[s][:, :, (b - 1) * NB:b * NB], [("out", s, b)]),
                      V(h.ap[:, :, b * NB:(b + 1) * NB], [("h", b)]), "out")
        P.barrier(barsc, full=True)

    def layer(s, l):
        need_ctx = l < DEPTH - 1
        colof = lambda blk: 2 if blk == 0 else s
        if STOP <= 0:
            return
        if s == 0:
            if l + 1 < NL:
                precast(l + 1)
            P.barrier(barsc)
            ps = PS[0]
            for g in range(12):
                w = wv(ws.get(l, "ada%d" % g), KC, 512)
                for j in range(4):
                    mc = g * 4 + j
                    for kc in range(KC):
                        P.mm(ps[:, mc * 3:mc * 3 + 3], w[:, kc, j * 128:(j + 1) * 128], scT[:, kc, :],
                             start=(kc == 0), stop=(kc == KC - 1))
            ps3 = V(ps.ap[:, 0:144].rearrange("p (m c) -> p m c", c=3), ps.keys)
            for c in range(3):
                P.tt("dve", modl[:, c, :], ps3[:, :, c], adab[:, l, :], ALU.add)
            for c in range(3):
                t = tmp()
                tv = V(t.ap[:, 0:8], t.keys)
                P.ts("dve", tv, modl[:, c, 8:16], 1.0, None, ALU.add)
                P.tt("dve", A1[:, l, c, :], tv, ng[:, l, 0, :], ALU.mult)
                P.copy("dve", B1[:, l, c, :], modl[:, c, 0:8])
                P.tt("dve", G1[:, l, c, :], modl[:, c, 16:24], ng[:, l, 1, :], ALU.mult)
                t = tmp()
                tv = V(t.ap[:, 0:8], t.keys)
                P.ts("dve", tv, modl[:, c, 32:40], 1.0, None, ALU.add)
                P.tt("dve", A2[:, l, c, :], tv, ng[:, l, 2, :], ALU.mult)
                P.copy("dve", B2[:, l, c, :], modl[:, c, 24:32])
                P.tt("dve", G2[:, l, c, :], modl[:, c, 40:48], ng[:, l, 3, :], ALU.mult)
        phase_begin(NB)
        ub = [M.alloc("ub%d" % i, [KC, NB], BF16) for i in range(2)]
        for blk in range(NBLK):
            dst = ub[blk % 2]
            norm_mod(blk * NB, NB, A1, B1, l, colof(blk), dst)
            P.dma("sp", V(ud[:, blk, :].rearrange("p (k t) -> p k t", k=KC), [("ud", blk)]),
                  V(dst.ap, [(dst.keys[0], k) for k in range(KC)]), "udw")
        if STOP <= 1:
            return
        for hp in range(2):
            gla_pair(s, l, hp)
        if STOP <= 2:
            return
        attn_c(s, l, need_ctx)
        if STOP <= 3:
            return
        attn_a_merge(s, l, need_ctx)
        if STOP <= 4:
            return
        ffn(s, l, need_ctx)

    def gla_pair(s, l, hp):
        phase_begin(NB, False, False)
        Q2 = M.alloc("Q2", [T], BF16)
        K2 = M.alloc("K2", [T], BF16)
        Ktm = M.alloc("Ktm", [18, 128], BF16)
        Vtm = M.alloc("Vtm", [18, 256], BF16)
        gb2 = M.alloc("gb2", [2, T], BF16)
        gtm = M.alloc("gtm", [18, 256], BF16)
        Sst = M.alloc("Sst", [18, 2, 2, 128], BF16)
        rTa = M.alloc("rTa", [NB], BF16)
        S2 = M.alloc("S2", [128], F32)
        S2b = M.alloc("S2b", [128], F32)
        Et = [M.alloc("Et%d" % i, [256], F32) for i in range(2)]
        Ent = [M.alloc("Ent%d" % i, [256], F32) for i in range(2)]
        ERt = [[M.alloc("ERt%d%d" % (d_, i), [128], F32) for i in range(2)] for d_ in range(2)]
        kend = [[M.alloc("kend%d%d" % (d_, i), [128], BF16) for i in range(2)] for d_ in range(2)]
        dec2 = [[M.alloc("dec%d%d" % (d_, i), [2], F32) for i in range(2)] for d_ in range(2)]
        qd = [M.alloc("qd%d" % i, [2, 128], BF16) for i in range(2)]
        kd = [M.alloc("kd%d" % i, [2, 128], BF16) for i in range(2)]
        attm = [M.alloc("attm%d" % i, [4, 128], BF16) for i in range(2)]
        obT = [M.alloc("obT%d" % i, [2, 128], F32) for i in range(2)]
        osq = [M.alloc("osq%d" % i, [2, 128], BF16) for i in range(2)]
        ybs = [M.alloc("ybs%d" % i, [4, NB], BF16) for i in range(1)]
        P.memset("pool", V(rTa.ap[32:64, :], rTa.keys), 1.0)
        for blk in range(NBLK):
            t0 = blk * NB
            u = load_u(blk)
            w1 = wv(ws.get(l, "g1_%d" % hp), KC, 288)
            ps = psr()
            proj_fm(w1, 0, 128, u, NB, ps)
            P.copy("act", V(Q2.ap[:, t0:t0 + NB], [("Q2", blk)]), ps[:, 0:NB])
            ps = psr()
            proj_fm(w1, 128, 128, u, NB, ps)
            P.copy("dve", V(K2.ap[:, t0:t0 + NB], [("K2", blk)]), ps[:, 0:NB])
            ps = psr()
            proj_fm(w1, 256, 32, u, NB, ps)
            P.copy("act", V(rTa.ap[0:32, :], rTa.keys), ps[0:32, 0:NB])
            for j in range(2):
                tb = blk * 2 + j
                ps = psr()
                proj_tm(w1, 128, 128, u, j * 128, ps)
                P.copy("dve", V(Ktm.ap[:, tb, :], [("Ktm", tb)]), ps[:, 0:128])
                ps = psr()
                P.mm(ps[:, 0:256], V(rTa.ap[0:64, j * 128:(j + 1) * 128], rTa.keys),
                     V(w2.ap[0:64, (l * 2 + hp) * 256:(l * 2 + hp + 1) * 256], w2.keys))
                t = tmp()
                tv = V(t.ap[:, 0:256], t.keys)
                P.act(tv, ps[:, 0:256], AF.Exp, scale=-1.0)
                t2 = tmp()
                tv2 = V(t2.ap[:, 0:256], t2.keys)
                P.act(tv2, tv, AF.Ln, bias=onesf[:, 0:1])
                P.ts("dve", V(gtm.ap[:, tb, :], [("gtm", tb)]), tv2, -1.0 / 16.0, None, ALU.mult)
            w2g = wv(ws.get(l, "g2_%d" % hp), KC, 512)
            for j in range(2):
                tb = blk * 2 + j
                ps = psr()
                proj_tm(w2g, 0, 256, u, j * 128, ps)
                P.copy("act", V(Vtm.ap[:, tb, :], [("Vtm", tb)]), ps[:, 0:256])
            for hh in range(2):
                ps = psr()
                proj_fm(w2g, 256 + hh * 128, 128, u, NB, ps)
                t = tmp()
                tv = V(t.ap[:, 0:NB], t.keys)
                P.act(tv, ps[:, 0:NB], AF.Exp, scale=-1.0)
                P.ts("dve", tv, tv, 1.0, None, ALU.add)
                P.recip(tv, tv)
                P.tt("dve", V(gb2.ap[:, hh, t0:t0 + NB], [("gb2", blk)]), tv, ps[:, 0:NB], ALU.mult)
        if STOP <= 1.3:
            return
        order_f = [(tb, c) for tb in range(18) for c in range(2)]
        order_b = [(tb, c) for tb in (1, 0) for c in (1, 0)] + [(tb, c) for tb in range(17, 1, -1) for c in (1, 0)]
        S2d = [S2, S2b]
        tbs_d = []
        for order in (order_f, order_b):
            tbs = []
            for (tb, c) in order:
                if not tbs or tbs[-1] != tb:
                    tbs.append(tb)
            tbs_d.append(tbs)
        for d in range(2):
            P.memset("dve", S2d[d], 0.0)

        def p1_local(d, tb):
            g_d = V(gtm.ap[:, tb, d * 128:(d + 1) * 128], [("gtm", tb)])
            psg = psr()
            P.mm(psg[:, 0:128], g_d, mat(6 + d))
            P.mm(psg[:, 128:256], mat(8 + d), g_d)
            er = ERt[d][tb % 2]
            P.act(er, psg[:, 128:256], AF.Exp)
            ke = kend[d][tb % 2]
            P.tt("dve", ke, V(Ktm.ap[:, tb, :], [("Ktm", tb)]), er, ALU.mult)
            dc = dec2[d][tb % 2]
            for cc in range(2):
                colx = cc * 64 + (63 if d == 0 else 0)
                P.act(dc[:, cc:cc + 1], psg[:, colx:colx + 1], AF.Exp)

        def p1_chain(d, tb, c):
            ke = kend[d][tb % 2]
            dc = dec2[d][tb % 2]
            S_ = S2d[d]
            P.copy("act", V(Sst.ap[:, tb, d, c, :], [("Sst", tb, d)]), S_)
            psu = psr()
            P.mm(psu[:, 0:256], V(ke.ap[c * 64:(c + 1) * 64, :], ke.keys),
                 V(Vtm.ap[c * 64:(c + 1) * 64, tb, :], [("Vtm", tb)]))
            for hh in range(2):
                P.stt("dve", V(S_.ap[hh * 64:(hh + 1) * 64, :], S_.keys), V(S_.ap[hh * 64:(hh + 1) * 64, :], S_.keys),
                      V(dc.ap[hh * 64:(hh + 1) * 64, c:c + 1], dc.keys),
                      psu[hh * 64:(hh + 1) * 64, hh * 128:(hh + 1) * 128], ALU.mult, ALU.add)

        for d in range(2):
            p1_local(d, tbs_d[d][0])
        for ti in range(18):
            for d in range(2):
                if ti + 1 < 18:
                    p1_local(d, tbs_d[d][ti + 1])
            for ci in range(2):
                for d in range(2):
                    c = ci if d == 0 else 1 - ci
                    p1_chain(d, tbs_d[d][ti], c)
        if STOP <= 1.6:
            return
        def p2_a(tb):
            blk = tb // 2
            t0 = tb * 128
            blk = tb // 2
            t0 = tb * 128
            E = Et[tb % 2]
            En = Ent[tb % 2]
            psg = psr()
            for d in range(2):
                P.mm(psg[:, d * 128:(d + 1) * 128], V(gtm.ap[:, tb, d * 128:(d + 1) * 128], [("gtm", tb)]), mat(6 + d))
            P.act(E, psg[:, 0:256], AF.Exp)
            P.act(En, psg[:, 0:256], AF.Exp, scale=-1.0)
            q_ = qd[tb % 2]
            k_ = kd[tb % 2]
            for d in range(2):
                P.tt("dve", q_[:, d, :], V(E.ap[:, d * 128:(d + 1) * 128], E.keys),
                     V(Q2.ap[:, t0:t0 + 128], [("Q2", blk)]), ALU.mult)
                P.tt("pool", k_[:, d, :], V(En.ap[:, d * 128:(d + 1) * 128], En.keys),
                     V(K2.ap[:, t0:t0 + 128], [("K2", blk)]), ALU.mult)

        def p2_b(tb):
            blk = tb // 2
            t0 = tb * 128
            q_ = qd[tb % 2]
            k_ = kd[tb % 2]
            psa2 = [psr(0, 4), psr(4, 8)]
            am = attm[tb % 2]
            for hh in range(2):
                for d in range(2):
                    P.mm(psa2[hh][:, d * 128:(d + 1) * 128], k_[hh * 64:(hh + 1) * 64, d, :], q_[hh * 64:(hh + 1) * 64, d, :])
            for hh in range(2):
                for d in range(2):
                    P.tt("dve", am[:, hh * 2 + d, :], mat(6 + d), psa2[hh][:, d * 128:(d + 1) * 128], ALU.mult)
            pso = psr()
            for hh in range(2):
                vt = V(Vtm.ap[:, tb, hh * 128:(hh + 1) * 128], [("Vtm", tb)])
                P.mm(pso[:, hh * 128:(hh + 1) * 128], vt, am[:, hh * 2, :], start=True, stop=False)
                P.mm(pso[:, hh * 128:(hh + 1) * 128], vt, am[:, hh * 2 + 1, :], start=False, stop=False)
                for c in range(2 if "sterm" not in SKIP else 0):
                    for d in range(2):
                        P.mm(pso[:, hh * 128 + c * 64:hh * 128 + (c + 1) * 64],
                             V(Sst.ap[hh * 64:(hh + 1) * 64, tb, d, c, :], [("Sst", tb, d)]),
                             q_[hh * 64:(hh + 1) * 64, d, c * 64:(c + 1) * 64],
                             start=False, stop=(c == 1 and d == 1))
            ob = obT[tb % 2]
            P.act(ob, V(pso.ap[:, 0:256].rearrange("p (a b) -> p a b", a=2), pso.keys), AF.Identity, scale=0.125)
            osq_ = osq[tb % 2]
            P.act(osq_, ob, AF.Square)
            psm = psr()
            P.mm(psm[:, 0:256], mat(2), V(osq_.ap.rearrange("p a b -> p (a b)"), osq_.keys))
            t = tmp()
            r = rstd_from(psm[:, 0:256], 256, out=V(t.ap[:, 0:256], t.keys))
            t2 = tmp()
            tv2 = V(t2.ap[:, 0:256].rearrange("p (a b) -> p a b", a=2), t2.keys)
            P.tt("dve", tv2, ob, V(r.ap.rearrange("p (a b) -> p a b", a=2), r.keys), ALU.mult)
            yb_ = ybs[0]
            j = tb % 2
            P.stt("dve", V(yb_.ap[:, 2 * hp:2 * hp + 2, j * 128:(j + 1) * 128], [(yb_.keys[0], j)]), tv2,
                  glag[:, l:l + 1], V(gb2.ap[:, :, t0:t0 + 128], [("gb2", blk)]), ALU.mult, ALU.mult)
            if j == 1 and "ydw" not in SKIP:
                P.dma("sp", V(yd[:, 1, blk, :].rearrange("p (k t) -> p k t", k=4)[:, 2 * hp:2 * hp + 2, :], [("yd", 1, blk, hp)]),
                      V(yb_.ap[:, 2 * hp:2 * hp + 2, :], [(yb_.keys[0], 0), (yb_.keys[0], 1)]), "ydw")


        p2_a(0)
        for tb in range(18):
            if tb + 1 < 18:
                p2_a(tb + 1)
            p2_b(tb)

    def rope_apply(src_bf, n, lt0, dst):
        psr_ = psr(5, 8)
        P.mm(psr_[:, 0:n], mat(5), src_bf)
        t1 = tmp()
        t1v = V(t1.ap[:, 0:n], t1.keys)
        P.tt("pool", t1v, src_bf, V(PH["rope"].ap[:, lt0:lt0 + n], PH["rope"].keys), ALU.mult)
        t2 = tmp()
        t2v = V(t2.ap[:, 0:n], t2.keys)
        P.tt("dve", t2v, psr_[:, 0:n], V(PH["rope"].ap[:, 2048 + lt0:2048 + lt0 + n], PH["rope"].keys), ALU.mult)
        P.tt("dve", dst, t1v, t2v, ALU.add)

    def attn_c(s, l, need_ctx):
        phase_begin(NB, True, False)
        kC = M.alloc("kC", [4, T], BF16)
        VC = M.alloc("VC", [18, 512], BF16)
        qr = M.alloc("qr", [4, NB], BF16)
        Es = [M.alloc("E%d" % i, [2, NB], BF16) for i in range(3)]
        raw = [M.alloc("raw%d" % i, [NB], BF16) for i in range(2)]
        rr = [M.alloc("rr%d" % i, [512], F32) for i in range(2)]
        oo = [M.alloc("oo%d" % i, [512], F32) for i in range(2)]
        oc = [M.alloc("oc%d" % i, [NB], F32) for i in range(2)]
        ocs = [M.alloc("ocs%d" % i, [NB], BF16) for i in range(2)]
        ycb = [M.alloc("ycb%d" % i, [4, NB], BF16) for i in range(2)]
        sacc = [[M.alloc("sacc%d%d" % (e_, m_), [2, NB], BF16) for m_ in range(2)] for e_ in range(2)]
        for blk in range(NBLK):
            t0 = blk * NB
            u = load_u(blk)
            w = wv(ws.get(l, "kvc1"), KC, 512)
            for c in range(4):
                ps = psr(0, 5)
                proj_fm(w, c * 128, 128, u, NB, ps)
                dst = V(kC.ap[:, c, t0:t0 + NB], [("kC", blk)])
                if blk == 0:
                    P.copy("act", dst, ps[:, 0:NB])
                else:
                    rw = raw[c % 2]
                    P.copy("act", rw, ps[:, 0:NB])
                    rope_apply(rw, NB, t0 - 256, dst)
            w = wv(ws.get(l, "kvc2"), KC, 512)
            for j in range(2):
                tb = blk * 2 + j
                ps = psr(0, 5)
                proj_tm(w, 0, 512, u, j * 128, ps)
                P.copy("act" if j == 0 else "dve", V(VC.ap[:, tb, :], [("VC", tb)]), ps[:, 0:512])
        P.barrier(barsc)
        for blk in range(NBLK):
            if blk == 0 and not need_ctx:
                continue
            t0 = blk * NB
            keys = [0, 1] if blk == 0 else list(range(18))
            u = load_u(blk)
            w = wv(ws.get(l, "qc"), KC, 512)
            for c in range(4):
                ps = PS[7]
                proj_fm(w, c * 128, 128, u, NB, ps)
                dst = V(qr.ap[:, c, :], [("qr", c)])
                if blk == 0:
                    P.copy("act", dst, ps[:, 0:NB])
                else:
                    rw = raw[c % 2]
                    P.copy("act", rw, ps[:, 0:NB])
                    rope_apply(rw, NB, t0 - 256, dst)
            yc_ = ycb[blk % 2]
            nkp = len(keys) // 2
            units = [(hd, m, kpi) for hd in range(4) for m in range(2) for kpi in range(nkp)]
            st = {}

            def c_s1(i):
                hd, m, kpi = units[i]
                off = m * 64
                pss = PS[i % 3]
                qv = V(qr.ap[off:off + 64, hd, :], [("qr", hd)])
                for j in range(2):
                    tb = keys[2 * kpi + j]
                    P.mm(pss[:, j * NB:(j + 1) * NB],
                         V(kC.ap[off:off + 64, hd, tb * 128:(tb + 1) * 128], [("kC", tb // 2)]), qv)
                E = Es[i % 3]
                P.act(E, V(pss.ap.rearrange("p (a b) -> p a b", a=2), pss.keys), AF.Exp, scale=0.125)

            def c_s3(i):
                hd, m, kpi = units[i]
                E = Es[i % 3]
                acc_o = PS[3 + hd % 2]
                acc_s = PS[5 + hd % 2]
                for j in range(2):
                    tb = keys[2 * kpi + j]
                    first = (kpi == 0 and j == 0)
                    last = (kpi == nkp - 1 and j == 1)
                    P.mm(acc_o[:, m * NB:(m + 1) * NB], V(VC.ap[:, tb, hd * 128:(hd + 1) * 128], [("VC", tb)]),
                         E[:, j, :], start=first, stop=last)
                eng = "dve" if kpi % 2 == 0 else "pool"
                sa = sacc[kpi % 2][m]
                if kpi < 2:
                    P.copy(eng, sa, E)
                else:
                    P.tt(eng, sa, sa, E, ALU.add)
                if m == 1 and kpi == nkp - 1:
                    for m2 in range(2):
                        srcs = [sacc[0][m2]] + ([sacc[1][m2]] if nkp > 1 else [])
                        nmm = 2 * len(srcs)
                        k_ = 0
                        for sa2 in srcs:
                            for j in range(2):
                                P.mm(acc_s[:, m2 * NB:(m2 + 1) * NB], mat(4), sa2[:, j, :], start=(k_ == 0), stop=(k_ == nmm - 1))
                                k_ += 1
                    r_ = rr[hd % 2]
                    o_ = oo[hd % 2]
                    P.recip(r_, acc_s)
                    P.tt("dve", o_, acc_o, r_, ALU.mult)
                    oc_ = oc[hd % 2]
                    P.stt("dve", oc_, V(o_.ap[:, NB:2 * NB], o_.keys), lamneg[:, l:l + 1], V(o_.ap[:, 0:NB], o_.keys),
                          ALU.mult, ALU.add)
                    os_ = ocs[hd % 2]
                    P.tt("pool", os_, oc_, oc_, ALU.mult)
                    psm = PS[7]
                    P.mm(psm[:, 0:NB], mat(2), os_)
                    t = tmp()
                    r = rstd_from(psm[:, 0:NB], NB, out=V(t.ap[:, 0:NB], t.keys))
                    P.stt("dve", V(yc_.ap[:, hd, :], [(yc_.keys[0], hd)]), oc_, dgs[:, l:l + 1], r, ALU.mult, ALU.mult)

            for i in range(len(units)):
                c_s1(i)
                if i >= 2:
                    c_s3(i - 2)
            c_s3(len(units) - 2)
            c_s3(len(units) - 1)
            P.dma("sp", V(yd[:, 2, blk, :].rearrange("p (k t) -> p k t", k=4), [("yd", 2, blk)]),
                  V(yc_.ap, [(yc_.keys[0], i) for i in range(4)]), "ydw")

    def attn_a_merge(s, l, need_ctx):
        phase_begin(NB, True)
        kA = M.alloc("kA", [T], BF16)
        VA = M.alloc("VA", [18, 2, 128], BF16)
        qr = M.alloc("qr", [4, NB], BF16)
        Es = [M.alloc("E%d" % i, [2, NB], BF16) for i in range(3)]
        kn = [M.alloc("kn%d" % i, [NB], BF16) for i in range(2)]
        ksq = [M.alloc("ksq%d" % i, [NB], BF16) for i in range(2)]
        rs = [M.alloc("rs%d" % i, [NB], F32) for i in range(2)]
        ya2 = [M.alloc("ya%d" % i, [4, NB], BF16) for i in range(2)]
        ybc2 = [M.alloc("ybc%d" % i, [2, 4, NB], BF16) for i in range(2)]
        pending = []
        mT = M.alloc("mT", [KC, NB], BF16)
        yT = M.alloc("yT", [KC, NB], F32)
        sg = [M.alloc("sg%d" % i, [NB], F32) for i in range(3)]
        macc = [M.alloc("macc%d" % i, [NB], F32) for i in range(2)]
        PH["mT"], PH["yT"], PH["sg"], PH["macc"] = mT, yT, sg, macc
        P.memset("pool", V(VA.ap[:, :, :, 64:128], [("VA", tb) for tb in range(18)]), 1.0)

        def qknorm(ps, which, is_lat, lt0, dst):
            i = _tc[0] % 2
            P.act(ksq[i], ps[:, 0:NB], AF.Square)
            psm = psr(5, 8)
            P.mm(psm[:, 0:NB], mat(3), ksq[i])
            t = tmp()
            r = rstd_from(psm[:, 0:NB], NB, out=V(t.ap[:, 0:NB], t.keys))
            if is_lat:
                P.stt("dve", kn[i], ps[:, 0:NB], qkg[:, l, which:which + 1], r, ALU.mult, ALU.mult)
                rope_apply(kn[i], NB, lt0, dst)
            else:
                P.stt("dve", dst, ps[:, 0:NB], qkg[:, l, which:which + 1], r, ALU.mult, ALU.mult)

        for blk in range(NBLK):
            t0 = blk * NB
            u = load_u(blk)
            w = wv(ws.get(l, "kva"), KC, 256)
            ps = psr(0, 5)
            proj_fm(w, 0, 128, u, NB, ps)
            qknorm(ps, 1, blk > 0, t0 - 256, V(kA.ap[:, t0:t0 + NB], [("kA", blk)]))
            for j in range(2):
                tb = blk * 2 + j
                ps = psr(0, 5)
                proj_tm(w, 128, 128, u, j * 128, ps)
                P.copy("act", V(VA.ap[:, tb, :, 0:64], [("VA", tb)]),
                       V(ps.ap[:, 0:128].rearrange("p (a b) -> p a b", a=2), ps.keys))
        P.barrier(barsc)
        for blk in range(NBLK):
            if blk == 0 and not need_ctx:
                continue
            t0 = blk * NB
            col = 2 if blk == 0 else s
            keys = [0, 1] if blk == 0 else list(range(18))
            u = load_u(blk)
            ya = ya2[blk % 2]
            ybc = ybc2[blk % 2]
            P.dma("sp", ybc, V(yd[:, 1:3, blk, :].rearrange("p a (k t) -> p a k t", k=4), ["ydr"]), ("ybc", blk % 2))
            w = wv(ws.get(l, "qa"), KC, 512)
            for c in range(4):
                ps = psr(5, 8)
                proj_fm(w, c * 128, 128, u, NB, ps)
                qknorm(ps, 0, blk > 0, t0 - 256, V(qr.ap[:, c, :], [("qr", c)]))
            nkp = len(keys) // 2
            units = [(c, kv, kpi) for c in range(4) for kv in range(2) for kpi in range(nkp)]

            def a_s1(i):
                c, kv, kpi = units[i]
                off = kv * 64
                pss = PS[i % 3]
                qv = V(qr.ap[off:off + 64, c, :], [("qr", c)])
                for j in range(2):
                    tb = keys[2 * kpi + j]
                    P.mm(pss[:, j * NB:(j + 1) * NB],
                         V(kA.ap[off:off + 64, tb * 128:(tb + 1) * 128], [("kA", tb // 2)]), qv)
                E = Es[i % 3]
                P.act(E, V(pss.ap.rearrange("p (a b) -> p a b", a=2), pss.keys), AF.Exp, scale=0.125)

            def a_s3(i):
                c, kv, kpi = units[i]
                E = Es[i % 3]
                acc = PS[3 + c % 2]
                for j in range(2):
                    tb = keys[2 * kpi + j]
                    P.mm(acc[:, kv * NB:(kv + 1) * NB], V(VA.ap[:, tb, kv, :], [("VA", tb)]), E[:, j, :],
                         start=(kpi == 0 and j == 0), stop=(kpi == nkp - 1 and j == 1))
                if kv == 1 and kpi == nkp - 1:
                    for kv2 in range(2):
                        r_ = rs[kv2]
                        P.recip(V(r_.ap[64:128, :], r_.keys), acc[64:128, kv2 * NB:(kv2 + 1) * NB])
                        po = (c % 2) * 64
                        P.tt("dve", V(ya.ap[po:po + 64, kv2 * 2 + c // 2, :], [(ya.keys[0], kv2 * 2 + c // 2, po)]),
                             acc[0:64, kv2 * NB:(kv2 + 1) * NB], V(r_.ap[64:128, :], r_.keys), ALU.mult)

            nun = len(units)
            nst = len(pending)
            done = 0
            for i in range(nun):
                a_s1(i)
                if i >= 2:
                    a_s3(i - 2)
                want = (i + 1) * nst // nun
                while done < want:
                    pending[done]()
                    done += 1
            a_s3(nun - 2)
            a_s3(nun - 1)
            while done < nst:
                pending[done]()
                done += 1
            pending = merge_steps(l, col, t0, u, ya, ybc)
        for st_ in pending:
            st_()

    def merge_steps(l, col, t0, u, ya, ybc):
        mT, yT, sg, macc = PH["mT"], PH["yT"], PH["sg"], PH["macc"]
        ysrc = [lambda kc: V(ya.ap[:, kc, :], [(ya.keys[0], kc, 0), (ya.keys[0], kc, 64)]),
                lambda kc: ybc[:, 0, kc, :], lambda kc: ybc[:, 1, kc, :]]
        st = {}
        steps = []

        def mstep(m, n):
            if n == 0:
                slot = ws.get(l, "mg%d" % m)
                st["wb"] = V(slot.ap[:, 0:1536].rearrange("p (n k c) -> p n k c", n=3, k=4), slot.keys)
                st["wg"] = V(slot.ap[:, 1536:4608].rearrange("p (n k c) -> p n k c", n=3, k=8), slot.keys)
            wb, wg = st["wb"], st["wg"]
            ma = macc[m % 2]
            psg = psr(5, 8)
            for kc in range(KC):
                P.mm(psg[:, 0:NB], wg[:, n, kc, :], u[:, kc, 0:NB], start=(kc == 0), stop=(kc == KC - 1))
            psb = psr(5, 8)
            for kc in range(4):
                P.mm(psb[:, 0:NB], wb[:, n, kc, :], ysrc[n](kc), start=(kc == 0), stop=(kc == 3))
            g_ = sg[n]
            P.act(g_, psg[:, 0:NB], AF.Exp, scale=-1.0)
            P.act(g_, g_, AF.Ln, bias=onesf[:, 0:1])
            P.act(g_, g_, AF.Exp, scale=-1.0)
            if n == 0:
                P.tt("dve", ma, g_, psb[:, 0:NB], ALU.mult)
            else:
                P.tt("dve", g_, g_, psb[:, 0:NB], ALU.mult)
                if n == 1:
                    P.tt("pool", ma, ma, g_, ALU.add)
                else:
                    P.tt("pool", V(mT.ap[:, m, :], [("mT", m)]), ma, g_, ALU.add)

        def wostep(mo):
            if mo % 4 == 0:
                st["wo"] = wv(ws.get(l, "wo%d" % (mo // 4)), KC, 512)
            wcur = st["wo"]
            ps = psr(5, 8)
            for kc in range(KC):
                P.mm(ps[:, 0:NB], wcur[:, kc, (mo % 4) * 128:(mo % 4 + 1) * 128], V(mT.ap[:, kc, :], [("mT", kc)]),
                     start=(kc == 0), stop=(kc == KC - 1))
            P.copy("act", V(yT.ap[:, mo, 0:NB], [("yT", mo)]), ps[:, 0:NB])
            P.act(V(PH["sq"].ap[:, mo, 0:NB], [("sq", mo)]), ps[:, 0:NB], AF.Square)

        def nstep():
            psm = psr(5, 8)
            for mo in range(8):
                P.mm(psm[:, 0:NB], mat(0), V(PH["sq"].ap[:, mo, 0:NB], [("sq", mo)]), start=(mo == 0), stop=(mo == 7))
            st["r"] = rstd_from(psm[:, 0:NB], NB)

        def rstep(mo):
            t = tmp()
            tv = V(t.ap[:, 0:NB], t.keys)
            P.stt("dve", tv, V(yT.ap[:, mo, 0:NB], [("yT", mo)]), G1[:, l, col, mo:mo + 1], st["r"], ALU.mult, ALU.mult)
            P.tt("pool", hk(mo, t0, NB), tv, hk(mo, t0, NB), ALU.add)

        for m in range(8):
            for n in range(3):
                steps.append(lambda m=m, n=n: mstep(m, n))
        for mo in range(8):
            steps.append(lambda mo=mo: wostep(mo))
        steps.append(nstep)
        for mo in range(8):
            steps.append(lambda mo=mo: rstep(mo))
        return steps

    def out_and_residual(l, col, t0, n, rhs_fn, nk, wnames, Gm, yT, matidx):
        for mo in range(8):
            if len(wnames) == 2:
                if mo % 4 == 0:
                    wcur = wv(ws.get(l, wnames[mo // 4]), KC, 512)
                lhs = lambda kc: wcur[:, kc, (mo % 4) * 128:(mo % 4 + 1) * 128]
            else:
                wcur = wv(ws.get(l, wnames[mo]), nk, 128)
                lhs = lambda kc: wcur[:, kc, :]
            ps = psr(0, 5)
            for kc in range(nk):
                P.mm(ps[:, 0:n], lhs(kc), rhs_fn(kc), start=(kc == 0), stop=(kc == nk - 1))
            P.copy("act", V(yT.ap[:, mo, 0:n], [("yT", mo)]), ps[:, 0:n])
            P.act(V(PH["sq"].ap[:, mo, 0:n], [("sq", mo)]), ps[:, 0:n], AF.Square)
        psm = psr(5, 8)
        for mo in range(8):
            P.mm(psm[:, 0:n], mat(0), V(PH["sq"].ap[:, mo, 0:n], [("sq", mo)]), start=(mo == 0), stop=(mo == 7))
        r = rstd_from(psm[:, 0:n], n)
        for mo in range(8):
            t = tmp()
            tv = V(t.ap[:, 0:n], t.keys)
            P.stt("dve", tv, V(yT.ap[:, mo, 0:n], [("yT", mo)]), Gm[:, l, col, mo:mo + 1], r, ALU.mult, ALU.mult)
            P.tt("pool", hk(mo, t0, n), tv, hk(mo, t0, n), ALU.add)

    def ffn(s, l, need_ctx):
        phase_begin(512)
        uT = PH["uT"]
        mid = M.alloc("mid", [FC, 512], BF16)
        yT = M.alloc("yTf", [KC, 512], F32)
        gb = [M.alloc("gbuf%d" % i, [516], F32) for i in range(3)]
        cb = [M.alloc("cb%d" % i, [512], F32) for i in range(3)]
        eb = [M.alloc("eb%d" % i, [512], F32) for i in range(3)]
        vbf = [M.alloc("vbf%d" % i, [512], F32) for i in range(3)]
        hh_ = M.alloc("hh", [KC, 8], F32)
        vh = M.alloc("vh", [KC, 8], BF16)
        ghalo = M.alloc("ghalo", [FC, 8], F32)
        blocks = [(0, 256, 2)] if need_ctx else []
        blocks += [(256 + i * 512, 512, s) for i in range(4)]
        halo_tok = [256 + 511, 256 + 512, 256 + 1023, 256 + 1024, 256 + 1535, 256 + 1536]
        for i, tk in enumerate(halo_tok):
            P.copy("dve", hh_[:, :, i:i + 1], V(h.ap[:, :, tk:tk + 1], [("h", tk // NB)]))
        for k in range(KC):
            P.act(V(PH["sq"].ap[:, k, 0:6], [("sq", k)]), hh_[:, k, 0:6], AF.Square)
        ps = psr(5, 8)
        for k in range(KC):
            P.mm(ps[:, 0:6], mat(0), V(PH["sq"].ap[:, k, 0:6], [("sq", k)]), start=(k == 0), stop=(k == KC - 1))
        r = rstd_from(ps[:, 0:6], 6)
        for k in range(KC):
            t = tmp()
            tv = V(t.ap[:, 0:6], t.keys)
            P.stt("dve", tv, hh_[:, k, 0:6], A2[:, l, s, k:k + 1], r, ALU.mult, ALU.mult)
            P.act(vh[:, k, 0:6], tv, AF.Identity, bias=B2[:, l, s, k:k + 1])
        for g in range(11):
            w = wv(ws.get(l, "up%d" % g), KC, 512)
            for jj in range(2):
                ps = psr(0, 5)
                for kc in range(KC):
                    P.mm(ps[:, 0:6], w[:, kc, jj * 256:jj * 256 + 128], vh[:, kc, 0:6], start=(kc == 0), stop=(kc == KC - 1))
                P.copy("act", ghalo[:, 2 * g + jj, 0:6], ps[:, 0:6])
        for bi, (t0, n, col) in enumerate(blocks):
            norm_mod(t0, n, A2, B2, l, col, uT)
            li = bi - (1 if need_ctx else 0)
            wst = {}

            def f_s1(j):
                g, jj = j // 2, j % 2
                if jj == 0:
                    wst["w"] = wv(ws.get(l, "up%d" % g), KC, 512)
                w = wst["w"]
                psg = psr(0, 3)
                psv = psr(3, 5)
                for kc in range(KC):
                    P.mm(psg[:, 0:n], w[:, kc, jj * 256:jj * 256 + 128], V(uT.ap[:, kc, 0:n], [("uT", kc)]),
                         start=(kc == 0), stop=(kc == KC - 1))
                for kc in range(KC):
                    P.mm(psv[:, 0:n], w[:, kc, jj * 256 + 128:jj * 256 + 256], V(uT.ap[:, kc, 0:n], [("uT", kc)]),
                         start=(kc == 0), stop=(kc == KC - 1))
                gbuf = gb[j % 3]
                P.copy("act", gbuf[:, 1:n + 1], psg[:, 0:n])
                vb_ = vbf[j % 3]
                P.copy("act", V(vb_.ap[:, 0:n], vb_.keys), psv[:, 0:n])
                if col != 2 and li > 0:
                    P.copy("pool", gbuf[:, 0:1], ghalo[:, j, 2 * li - 2:2 * li - 1])
                else:
                    P.memset("pool", gbuf[:, 0:1], 0.0)
                if col != 2 and li < 3:
                    P.copy("pool", gbuf[:, n + 1:n + 2], ghalo[:, j, 2 * li + 1:2 * li + 2])
                else:
                    P.memset("pool", gbuf[:, n + 1:n + 2], 0.0)
                c_ = cb[j % 3]
                cv = V(c_.ap[:, 0:n], c_.keys)
                P.ts("dve", cv, gbuf[:, 1:n + 1], convw[:, l, j, 1:2], convb[:, l, j:j + 1], ALU.mult, ALU.add)
                P.stt("dve", cv, gbuf[:, 0:n], convw[:, l, j, 0:1], cv, ALU.mult, ALU.add)
                P.stt("dve", cv, gbuf[:, 2:n + 2], convw[:, l, j, 2:3], cv, ALU.mult, ALU.add)

            def f_s2(j):
                c_ = cb[j % 3]
                cv = V(c_.ap[:, 0:n], c_.keys)
                vb_ = vbf[j % 3]
                vb = V(vb_.ap[:, 0:n], vb_.keys)
                e_ = eb[j % 3]
                ev = V(e_.ap[:, 0:n], e_.keys)
                P.act(ev, cv, AF.Exp, scale=-1.0)
                P.act(ev, ev, AF.Ln, bias=onesf[:, 0:1])
                P.act(ev, ev, AF.Exp, scale=-1.0)
                P.tt("pool", ev, ev, cv, ALU.mult)
                P.tt("pool", V(mid.ap[:, j, 0:n], [("mid", j)]), ev, vb, ALU.mult)

            f_s1(0)
            for j in range(FC):
                if j + 1 < FC:
                    f_s1(j + 1)
                f_s2(j)
            out_and_residual(l, col, t0, n, lambda kc: V(mid.ap[:, kc, 0:n], [("mid", kc)]), FC,
                             tuple("dn%d" % m for m in range(8)), G2, yT, 0)

    P.dry = True
    body()
    P.dry = False
    P.ops = []
    _tc[0] = 0
    _pc[0] = 0
    body()
    P.emit(nc)
    return nc


_CACHE = {}


def kernel(**inputs):
    inp = {k: np.asarray(v) for k, v in inputs.items()}
    B = inp["x"].shape[0]
    ncore = 8
    per = B // ncore
    sm, cmat, rope, w2 = prep_consts(inp)
    W = np.stack([prep_layer(inp, l) for l in range(DEPTH)], axis=0)
    if "nc" not in _CACHE:
        _CACHE["nc"] = build(DEPTH, per)
    nc = _CACHE["nc"]
    in_maps = []
    for c in range(ncore):
        xs = inp["x"][c * per:(c + 1) * per]
        xT = np.ascontiguousarray(xs.reshape(per, 2048, KC, 128).transpose(0, 3, 2, 1))
        cs = inp["ctx"][c * per:(c + 1) * per]
        cxT = np.ascontiguousarray(cs.reshape(per, 256, KC, 128).transpose(0, 3, 2, 1))
        smc = sm.copy()
        cc = np.concatenate([inp["c"][c * per:(c + 1) * per], inp["c_ctx"][None, :]], axis=0)
        if per == 1:
            cc = np.concatenate([cc[0:1], cc[0:1], cc[1:2]], axis=0)
        o, e = SM["cT"]
        smc[:, o:o + e] = cc.reshape(3, KC, 128).transpose(2, 1, 0).reshape(128, e)
        in_maps.append({"xT": xT, "cxT": cxT, "smallc": smc, "cmat": cmat, "rope": rope, "w2aug": w2, "W": W})
    res = run_bass_kernel_spmd(nc, in_maps, core_ids=list(range(ncore)))
    outs = []
    for c in range(ncore):
        o = np.asarray(res.results[c]["out"])
        outs.append(o.transpose(0, 3, 2, 1).reshape(per, 2048, D))
    return np.concatenate(outs, axis=0).astype(np.float32)
```
